# Optimizing a Trainium2 kernel written in Bass

```python
import jax, jax.numpy as jnp
from jax import lax
import numpy as np

D_MODEL = 1024
BATCH = 4
SEQ = 4096
DEPTH = 1

PLE_DIM = 256
N_HEADS = 8
N_KV = 2
HPG = N_HEADS // N_KV
HEAD_DIM = 64
ROT_DIM = HEAD_DIM // 4
ROPE_THETA = 500000.0
CMP_BLOCK = 32
CMP_STRIDE = 16
CMP_HIDDEN = 128
SEL_BLOCK = 64
SEL_TOPK = 16
WINDOW = 512
Q_BLOCK = 128
CONV_CH = 512
CONV_WIDTH = 31
FFN_DIM = 2816
FFN_CONV_WIDTH = 3
EPS = 1e-6
NEG = -1e30
FORCED_SCORE = 1e6

ATT_W = N_HEADS * HEAD_DIM
KV_W = N_KV * HEAD_DIM
IN_COLS = ATT_W + 6 * KV_W + 3 * N_HEADS + 2 * CONV_CH + 2 * D_MODEL

kernel_name = "hybrid_nsa_conformer_convffn_block"


def rms_norm(x, g):
    xf = x.astype(jnp.float32)
    y = xf * lax.rsqrt(jnp.mean(xf * xf, axis=-1, keepdims=True) + EPS)
    return (y * g.astype(jnp.float32)).astype(x.dtype)


def layer_norm(x, g, b):
    xf = x.astype(jnp.float32)
    mu = jnp.mean(xf, axis=-1, keepdims=True)
    var = jnp.mean(jnp.square(xf - mu), axis=-1, keepdims=True)
    y = (xf - mu) * lax.rsqrt(var + EPS)
    return (y * g.astype(jnp.float32) + b.astype(jnp.float32)).astype(x.dtype)


def rope_partial(x, pos):
    half = ROT_DIM // 2
    inv = ROPE_THETA ** (-jnp.arange(0, ROT_DIM, 2, dtype=jnp.float32) / ROT_DIM)
    ang = pos.astype(jnp.float32)[..., None] * inv
    ang = ang.reshape((pos.shape[0],) + (1,) * (x.ndim - 3) + (pos.shape[1], half))
    cos = jnp.cos(ang).astype(x.dtype)
    sin = jnp.sin(ang).astype(x.dtype)
    x1 = x[..., :half]
    x2 = x[..., half:ROT_DIM]
    return jnp.concatenate([x1 * cos - x2 * sin, x2 * cos + x1 * sin, x[..., ROT_DIM:]], axis=-1)


def causal_dwconv(x, w, b):
    k = w.shape[0]
    y = lax.conv_general_dilated(x, w.astype(x.dtype), window_strides=(1,), padding=((k - 1, 0),),
                                 dimension_numbers=('NWC', 'WIO', 'NWC'), feature_group_count=x.shape[-1])
    return y + b.astype(x.dtype)


def masked_softmax(s, mask):
    return jax.nn.softmax(jnp.where(mask, s.astype(jnp.float32), NEG), axis=-1)


def nsa_attention(q, kc, vc, ks, vs, kw, vw, gate_logits, positions, pe_k, pe_v, ck1, ck2, cv1, cv2):
    B, S = q.shape[0], q.shape[1]
    q = q.reshape(B, S, N_KV, HPG, HEAD_DIM).transpose(0, 2, 3, 1, 4)
    heads = lambda t: t.reshape(B, S, N_KV, HEAD_DIM).transpose(0, 2, 1, 3)
    kc, vc, ks, vs, kw, vw = heads(kc), heads(vc), heads(ks), heads(vs), heads(kw), heads(vw)
    q = rope_partial(q, positions)
    ks = rope_partial(ks, positions)
    kw = rope_partial(kw, positions)

    n_cmp = (S - CMP_BLOCK) // CMP_STRIDE + 1
    cmp_start = jnp.arange(n_cmp) * CMP_STRIDE
    cmp_idx = cmp_start[:, None] + jnp.arange(CMP_BLOCK)[None, :]
    cmp_end = cmp_start + CMP_BLOCK - 1

    def compress(t, pe, w1, w2):
        blk = (t[:, :, cmp_idx] + pe).reshape(B, N_KV, n_cmp, CMP_BLOCK * HEAD_DIM)
        return jax.nn.gelu(blk @ w1) @ w2

    k_cmp = rope_partial(compress(kc, pe_k, ck1, ck2), positions[:, cmp_end])
    v_cmp = compress(vc, pe_v, cv1, cv2)

    n_sel = S // SEL_BLOCK
    top_n = min(SEL_TOPK, n_sel)
    k_sel = ks.reshape(B, N_KV, n_sel, SEL_BLOCK, HEAD_DIM)
    v_sel = vs.reshape(B, N_KV, n_sel, SEL_BLOCK, HEAD_DIM)
    sel_start = jnp.arange(n_sel) * SEL_BLOCK
    overlap = jnp.clip(jnp.minimum(cmp_start[:, None] + CMP_BLOCK, sel_start[None, :] + SEL_BLOCK)
                       - jnp.maximum(cmp_start[:, None], sel_start[None, :]), 0).astype(jnp.float32) / CMP_STRIDE

    kw_pad = jnp.pad(kw, ((0, 0), (0, 0), (WINDOW, 0), (0, 0)))
    vw_pad = jnp.pad(vw, ((0, 0), (0, 0), (WINDOW, 0), (0, 0)))

    gates = jax.nn.sigmoid(gate_logits).reshape(B, S, 3, N_KV, HPG).transpose(2, 0, 3, 4, 1)
    scale = HEAD_DIM ** -0.5
    bi = jnp.arange(B)[:, None, None, None]
    gi = jnp.arange(N_KV)[None, :, None, None]
    j_sel = jnp.arange(n_sel)

    def chunk(c):
        start = c * Q_BLOCK
        qc = lax.dynamic_slice_in_dim(q, start, Q_BLOCK, axis=3) * scale
        gc = lax.dynamic_slice_in_dim(gates, start, Q_BLOCK, axis=4)
        t = start + jnp.arange(Q_BLOCK)

        s = jnp.einsum('bghqd,bgnd->bghqn', qc, k_cmp)
        valid = cmp_end[None, :] <= t[:, None]
        p_cmp = jnp.where(valid, masked_softmax(s, valid), 0.0)
        o_cmp = jnp.einsum('bghqn,bgnd->bghqd', p_cmp.astype(v_cmp.dtype), v_cmp)

        imp = jnp.einsum('bghqn,nj->bgqj', p_cmp, overlap)
        cur = t // SEL_BLOCK
        forced = (j_sel[None] == 0) | (j_sel[None] == cur[:, None]) | (j_sel[None] == cur[:, None] - 1)
        blk_valid = j_sel[None] * SEL_BLOCK <= t[:, None]
        score = jnp.where(blk_valid, jnp.where(forced, FORCED_SCORE, imp), -1.0)
        _, idx = lax.top_k(score, top_n)
        kg = k_sel[bi, gi, idx]
        vg = v_sel[bi, gi, idx]
        s = jnp.einsum('bghqd,bgqnld->bghqnl', qc, kg)
        tok = idx[..., None] * SEL_BLOCK + jnp.arange(SEL_BLOCK)
        tmask = (tok <= t[:, None, None])[:, :, None].reshape(B, N_KV, 1, Q_BLOCK, top_n * SEL_BLOCK)
        p = masked_softmax(s.reshape(B, N_KV, HPG, Q_BLOCK, top_n * SEL_BLOCK), tmask)
        p = p.reshape(B, N_KV, HPG, Q_BLOCK, top_n, SEL_BLOCK).astype(vg.dtype)
        o_sel = jnp.einsum('bghqnl,bgqnld->bghqd', p, vg)

        kwc = lax.dynamic_slice_in_dim(kw_pad, start, Q_BLOCK + WINDOW, axis=2)
        vwc = lax.dynamic_slice_in_dim(vw_pad, start, Q_BLOCK + WINDOW, axis=2)
        spos = start - WINDOW + jnp.arange(Q_BLOCK + WINDOW)
        wmask = (spos[None] >= 0) & (spos[None] <= t[:, None]) & (t[:, None] - spos[None] < WINDOW)
        s = jnp.einsum('bghqd,bgkd->bghqk', qc, kwc)
        p = masked_softmax(s, wmask).astype(vwc.dtype)
        o_win = jnp.einsum('bghqk,bgkd->bghqd', p, vwc)

        o = gc[0][..., None] * o_cmp + gc[1][..., None] * o_sel + gc[2][..., None] * o_win
        return o.transpose(0, 3, 1, 2, 4).reshape(B, Q_BLOCK, ATT_W)

    out = lax.map(chunk, jnp.arange(S // Q_BLOCK))
    return out.transpose(1, 0, 2, 3).reshape(B, S, ATT_W)


def conformer_conv(z, dw_w, dw_b, ln_g, ln_b):
    a, b = jnp.split(z, 2, axis=-1)
    y = causal_dwconv(a * jax.nn.sigmoid(b), dw_w, dw_b)
    return jax.nn.silu(layer_norm(y, ln_g, ln_b))


def setup_inputs(seed: int = 0) -> dict:
    key = jax.random.key(seed)
    ks = jax.random.split(key, 32)
    f32 = jnp.float32
    nrm = lambda k, shape, sc: jax.random.normal(k, shape, f32) * sc
    gain = lambda k, shape: 1.0 + 0.01 * jax.random.normal(k, shape, f32)
    L = DEPTH
    return {
        "x": jax.random.normal(ks[0], (BATCH, SEQ, D_MODEL), f32),
        "p": jax.random.normal(ks[1], (DEPTH, BATCH, SEQ, PLE_DIM), f32),
        "positions": (jnp.arange(SEQ, dtype=jnp.int32)[None, :]
                      + jax.random.randint(ks[2], (BATCH, 1), 0, 1024, dtype=jnp.int32)),
        "norm_mix_g": gain(ks[3], (L, D_MODEL)),
        "w_in": nrm(ks[4], (L, D_MODEL, IN_COLS), D_MODEL ** -0.5),
        "pe_k": nrm(ks[5], (L, CMP_BLOCK, HEAD_DIM), 0.5),
        "pe_v": nrm(ks[6], (L, CMP_BLOCK, HEAD_DIM), 0.5),
        "cmp_k_w1": nrm(ks[7], (L, CMP_BLOCK * HEAD_DIM, CMP_HIDDEN), (CMP_BLOCK * HEAD_DIM) ** -0.5),
        "cmp_k_w2": nrm(ks[8], (L, CMP_HIDDEN, HEAD_DIM), CMP_HIDDEN ** -0.5),
        "cmp_v_w1": nrm(ks[9], (L, CMP_BLOCK * HEAD_DIM, CMP_HIDDEN), (CMP_BLOCK * HEAD_DIM) ** -0.5),
        "cmp_v_w2": nrm(ks[10], (L, CMP_HIDDEN, HEAD_DIM), CMP_HIDDEN ** -0.5),
        "conv_dw_w": nrm(ks[11], (L, CONV_WIDTH, 1, CONV_CH), CONV_WIDTH ** -0.5),
        "conv_dw_b": nrm(ks[12], (L, CONV_CH), 0.01),
        "conv_ln_g": gain(ks[13], (L, CONV_CH)),
        "conv_ln_b": nrm(ks[14], (L, CONV_CH), 0.01),
        "w_a": nrm(ks[15], (L, ATT_W, D_MODEL), ATT_W ** -0.5),
        "w_b": nrm(ks[16], (L, CONV_CH, D_MODEL), CONV_CH ** -0.5),
        "w_o": nrm(ks[17], (L, D_MODEL, D_MODEL), D_MODEL ** -0.5),
        "norm_ffn_g": gain(ks[18], (L, D_MODEL)),
        "w_up": nrm(ks[19], (L, D_MODEL, 2 * FFN_DIM), D_MODEL ** -0.5),
        "ffn_dw_w": nrm(ks[20], (L, FFN_CONV_WIDTH, 1, 2 * FFN_DIM), FFN_CONV_WIDTH ** -0.5),
        "ffn_dw_b": nrm(ks[21], (L, 2 * FFN_DIM), 0.01),
        "w_down": nrm(ks[22], (L, FFN_DIM, D_MODEL), FFN_DIM ** -0.5),
        "norm_ple_g": gain(ks[23], (L, D_MODEL)),
        "w_ple_gate": nrm(ks[24], (L, D_MODEL, D_MODEL), D_MODEL ** -0.5),
        "w_ple_proj": nrm(ks[25], (L, PLE_DIM, D_MODEL), PLE_DIM ** -0.5),
        "norm_final_g": gain(ks[26], (D_MODEL,)),
    }


def reference(x, p, positions, norm_mix_g, w_in, pe_k, pe_v, cmp_k_w1, cmp_k_w2, cmp_v_w1, cmp_v_w2,
              conv_dw_w, conv_dw_b, conv_ln_g, conv_ln_b, w_a, w_b, w_o, norm_ffn_g, w_up, ffn_dw_w,
              ffn_dw_b, w_down, norm_ple_g, w_ple_gate, w_ple_proj, norm_final_g):
    sizes = [ATT_W] + [KV_W] * 6 + [3 * N_HEADS, 2 * CONV_CH, 2 * D_MODEL]
    offsets = np.cumsum(sizes)[:-1].tolist()
    h = x
    for i in range(DEPTH):
        u = rms_norm(h, norm_mix_g[i])
        z = u @ w_in[i]
        q, kc, vc, ks_, vs_, kw, vw, nsa_g, glu_in, merge_g = jnp.split(z, offsets, axis=-1)
        y_a = nsa_attention(q, kc, vc, ks_, vs_, kw, vw, nsa_g, positions, pe_k[i], pe_v[i],
                            cmp_k_w1[i], cmp_k_w2[i], cmp_v_w1[i], cmp_v_w2[i]) @ w_a[i]
        y_b = conformer_conv(glu_in, conv_dw_w[i], conv_dw_b[i], conv_ln_g[i], conv_ln_b[i]) @ w_b[i]
        g_a, g_b = jnp.split(merge_g, 2, axis=-1)
        h = h + (jax.nn.sigmoid(g_a) * y_a + jax.nn.sigmoid(g_b) * y_b) @ w_o[i]
        u = rms_norm(h, norm_ffn_g[i])
        up = causal_dwconv(u @ w_up[i], ffn_dw_w[i], ffn_dw_b[i])
        a, g = jnp.split(up, 2, axis=-1)
        h = h + (jax.nn.silu(g) * a) @ w_down[i]
        u = rms_norm(h, norm_ple_g[i])
        h = h + jax.nn.sigmoid(u @ w_ple_gate[i]) * (p[i] @ w_ple_proj[i])
    return rms_norm(h, norm_final_g)
```

```python
import numpy as np
import ml_dtypes
from contextlib import ExitStack
import concourse.bass as bass
import concourse.mybir as mybir
from concourse.bass_utils import run_bass_kernel_spmd

F32 = mybir.dt.float32
BF16 = mybir.dt.bfloat16
I32 = mybir.dt.int32
ALU = mybir.AluOpType
AF = mybir.ActivationFunctionType

NQT = 17
EPS = 1e-6
TWO_PI = float(2 * np.pi)


class Sched:
    ENGS = ("pe", "act", "dve", "pool", "sp")

    def __init__(self):
        self.ops = []
        self.lastw = {}
        self.readers = {}
        self.dma_readers = {}
        self.lane_last = {}
        self.psum_acc = {}

    max_ops = 10 ** 9

    def add(self, eng, fn, reads=(), writes=(), lane=None, nochain=False):
        idx = len(self.ops)
        if idx >= self.max_ops:
            return
        deps = set()
        for k in reads:
            for i in self.lastw.get(k, {}).values():
                deps.add(i)
        for k in writes:
            for i in self.lastw.get(k, {}).values():
                deps.add(i)
            for i in self.readers.get(k, {}).values():
                deps.add(i)
            for i in self.dma_readers.get(k, ()):
                deps.add(i)
        if lane is not None and lane in self.lane_last and not nochain:
            deps.add(self.lane_last[lane])
        pk_ = [k for k in list(reads) + list(writes) if isinstance(k, tuple) and k[0] in ("pT", "pP", "pAcc")]
        for k in pk_:
            for e2, i in self.psum_acc.setdefault(k, {}).items():
                if e2 != eng:
                    deps.add(i)
            self.psum_acc[k][eng] = idx
        pruned = set()
        for d in deps:
            p = self.ops[d]
            if p["lane"] is None and lane is None and p["eng"] == eng and eng == "pe":
                continue
            pruned.add(d)
        self.ops.append(dict(eng=eng, fn=fn, deps=sorted(pruned), lane=lane, has_dep=False))
        for d in pruned:
            self.ops[d]["has_dep"] = True
        for k in writes:
            self.lastw.setdefault(k, {})[eng if lane is None else ("dma", lane)] = idx
            self.readers[k] = {}
            self.dma_readers[k] = []
        for k in reads:
            if lane is None:
                self.readers.setdefault(k, {})[eng] = idx
            else:
                self.dma_readers.setdefault(k, []).append(idx)
        if lane is not None:
            self.lane_last[lane] = idx
        return idx

    def emit(self, nc, final_wait_lanes=()):
        ops = self.ops
        lanes = []
        for o in ops:
            if o["lane"] is not None and o["lane"] not in lanes:
                lanes.append(o["lane"])
        with ExitStack() as st:
            esem = {e: st.enter_context(nc.semaphore("s_" + e)) for e in self.ENGS}
            lsem = {l: st.enter_context(nc.semaphore("l_%d" % i)) for i, l in enumerate(lanes)}
            ecnt = {e: 0 for e in self.ENGS}
            lcnt = {l: 0 for l in lanes}
            for o in ops:
                if o["lane"] is not None:
                    lcnt[o["lane"]] += 16
                    o["sem"], o["val"], o["inc"] = lsem[o["lane"]], lcnt[o["lane"]], 16
                elif o["has_dep"]:
                    ecnt[o["eng"]] += 1
                    o["sem"], o["val"], o["inc"] = esem[o["eng"]], ecnt[o["eng"]], 1
            final_vals = {l: lcnt[l] for l in final_wait_lanes if l in lcnt}
            block = st.enter_context(nc.Block())

            def run_engine(ename, eh):
                waited = {}
                for o in ops:
                    if o["eng"] != ename:
                        continue
                    for d in o["deps"]:
                        p = ops[d]
                        s, v = p["sem"], p["val"]
                        if waited.get(id(s), 0) >= v:
                            continue
                        eh.wait_ge(s, v)
                        waited[id(s)] = v
                    ins = o["fn"](eh)
                    if o["lane"] is not None or o["has_dep"]:
                        ins.then_inc(o["sem"], o["inc"])
                if ename == "sp":
                    for l, v in final_vals.items():
                        eh.wait_ge(lsem[l], v)

            @block.tensor
            def _(e):
                run_engine("pe", e)

            @block.scalar
            def _(e):
                run_engine("act", e)

            @block.vector
            def _(e):
                run_engine("dve", e)

            @block.gpsimd
            def _(e):
                run_engine("pool", e)

            @block.sync
            def _(e):
                run_engine("sp", e)


def I(name, *args, **kw):
    return lambda e: getattr(e, name)(*args, **kw)


class Rot:
    def __init__(self, tiles, name, fixed=None, keys=None):
        self.tiles = tiles
        self.name = name
        self.fixed = fixed
        self.keys = keys
        self.i = 0

    def next(self):
        t = self.tiles[self.i % len(self.tiles)]
        k = (self.name, self.i % len(self.tiles)) if self.fixed is None else self.fixed
        if self.keys is not None:
            k = self.keys[self.i % len(self.tiles)]
        self.i += 1
        return t, k


def build(nc, dbg_names=(), stop=99):
    S = Sched()
    A = S.add
    st = ExitStack()
    DI = lambda n, s, d: nc.dram_tensor(n, s, d, kind="ExternalInput").ap()
    xc = DI("xc", [4096, 1024], F32)
    posr = DI("posr", [1, 4096], I32)
    pmine = DI("pmine", [2048, 256], F32)
    posc = DI("posc", [1, 255], I32)
    w_in = DI("w_in", [128, 8, 4376], F32)
    g_mix = DI("g_mix", [1, 1024], F32)
    g_ffn = DI("g_ffn", [1, 1024], F32)
    g_ple = DI("g_ple", [1, 1024], F32)
    g_fin = DI("g_fin", [1, 1024], F32)
    w1k = DI("w1k", [64, 32, 128], F32)
    w1v = DI("w1v", [64, 32, 128], F32)
    w2k = DI("w2k", [128, 64], F32)
    w2v = DI("w2v", [128, 64], F32)
    pek = DI("pek", [64, 32], F32)
    pev = DI("pev", [64, 32], F32)
    cdw = DI("cdw", [128, 4, 34], F32)
    wmrg = DI("wmrg", [128, 8, 3072], F32)
    w_o = DI("w_o", [128, 8, 1024], F32)
    w_up = DI("w_up", [128, 22, 2048], F32)
    fdw = DI("fdw", [128, 44, 4], F32)
    w_dn = DI("w_dn", [128, 2, 22, 512], F32)
    w_pg = DI("w_pg", [128, 8, 1024], F32)
    w_pp = DI("w_pp", [128, 2, 1024], F32)
    ident = DI("ident", [128, 128], BF16)
    pmbd = DI("pmbd", [128, 128], BF16)
    ropec = DI("ropec", [128, 2], F32)
    emat = DI("emat", [64, 4096], BF16)
    ovm = DI("ovm", [128, 2, 65], BF16)
    tabs = DI("tabs", [NQT, 128, 3, 64], F32)
    wmask = DI("wmask", [NQT, 128, 5, 128], BF16)
    cmaskd = DI("cmaskd", [NQT, 128, 2, 128], BF16)
    flagd = DI("flagd", [128, 1], F32)
    out = nc.dram_tensor("out", [2048, 1024], F32, kind="ExternalOutput").ap()
    h1s = nc.dram_tensor("h1s", [NQT * 128, 1024], F32, kind="Internal").ap()
    SCR = lambda n_, sh: nc.dram_tensor(n_, sh, BF16, kind="Internal").ap()
    w_in_b = SCR("w_in_b", [128, 8, 4376])
    wmrg_b = SCR("wmrg_b", [128, 8, 3072])
    w_o_b = SCR("w_o_b", [128, 8, 1024])
    w_up_b = SCR("w_up_b", [128, 22, 2048])
    w_dn_b = SCR("w_dn_b", [128, 2, 22, 512])
    w_pg_b = SCR("w_pg_b", [128, 8, 1024])
    w_pp_b = SCR("w_pp_b", [128, 2, 1024])
    w1k_b, w1v_b = SCR("w1k_b", [64, 32, 128]), SCR("w1v_b", [64, 32, 128])
    w2k_b, w2v_b = SCR("w2k_b", [128, 64]), SCR("w2v_b", [128, 64])
    pek_b, pev_b = SCR("pek_b", [64, 32]), SCR("pev_b", [64, 32])
    dbg = {n: nc.dram_tensor("dbg_" + n, s, F32, kind="ExternalOutput").ap() for n, s in dbg_names}

    T = lambda name, shape, dt: st.enter_context(nc.sbuf_tensor(name, shape, dt))
    PS = lambda name, shape, dt: st.enter_context(nc.psum_tensor(name, shape, dt))

    def rot(name, n, shape, dt, psum=False):
        return Rot([(PS if psum else T)("%s%d" % (name, i), shape, dt) for i in range(n)], name)

    def dump(name, ap, key):
        if name in dbg:
            A("pool", I("dma_start", out=dbg[name], in_=ap), reads=[key], lane="dbg_" + name)

    def finish_build():
        print("NOPS", len(S.ops))
        S.emit(nc, final_wait_lanes=["ost"] + ["dbg_" + n_ for n_ in dbg])
        st.close()
        return nc

    pT = rot("pT", 1, [128, 1024], BF16, psum=True)
    pP = rot("pP", 2, [128, 512], F32, psum=True)
    pAcc = rot("pAcc", 5, [128, 512], F32, psum=True)
    pM = Rot(pP.tiles + pAcc.tiles, "pM", keys=[("pP", i) for i in range(2)] + [("pAcc", i) for i in range(5)])
    ACC = [(pAcc.tiles[i], ("pAcc", i)) for i in range(5)]
    conv_bank = [None]

    idt = T("idt", [128, 128], BF16)
    pmb = T("pmb", [128, 128], BF16)
    rpc = T("rpc", [128, 2], F32)
    ovt = T("ovt", [128, 2, 65], BF16)
    flg = T("flg", [128, 1], F32)
    cdwt = T("cdwt", [128, 4, 34], F32)
    fdwt = T("fdwt", [128, 44, 4], F32)
    onesf = T("onesf", [128, 128], F32)
    halfpi = T("halfpi", [128, 1], F32)
    ld = 0

    def load(dst, src, key, eng="sp"):
        nonlocal ld
        ld += 1
        A(eng, I("dma_start", out=dst, in_=src), writes=[key], lane="ld%d" % (ld % 4) if eng == "sp" else "ldp%d" % (ld % 4))

    load(idt[:], ident, "idt")
    load(pmb[:], pmbd, "pmb")
    load(rpc[:], ropec, "rpc")
    load(ovt[:], ovm, "ovt")
    load(flg[:], flagd, "flg")
    load(cdwt[:], cdw, "cdwt")
    load(fdwt[:], fdw, "fdwt")
    A("dve", I("memset", onesf[:], 1.0 / 512), writes=["onesf"])
    A("dve", I("memset", halfpi[:], float(np.pi / 2)), writes=["halfpi"])

    ksE = [T("ksE%d" % g, [128, 4096], BF16) for g in range(2)]
    kwT = T("kwT", [128, 4096], BF16)
    VallF = T("VallF", [128, 32 * 4 * 65], BF16)
    Vall = VallF[:].rearrange("p (a b c) -> p a b c", a=32, b=4)
    arena1 = T("arena1", [128, 22 * 512], BF16)
    kcT = arena1[:, 0:4096]
    vcT = arena1[:, 4096:8192]
    kcmpT = T("kcmpT", [128, 256], BF16)
    Vc = T("Vc", [128, 2, 2, 65], BF16)
    for g in range(2):
        load(ksE[g][64:128, :], emat, ("ksE", g))
    A("dve", I("memset", Vall[:], 1.0), writes=["Vall"])
    A("dve", I("memset", Vc[:], 0.0), writes=["Vc"])
    A("dve", I("memset", Vc[:, :, :, 64:65], 1.0), writes=["Vc"])
    A("dve", I("memset", kcmpT[:], 0.0), writes=["kcmpT"])

    wbuf = rot("wbuf", 3, [128, 8 * 512], BF16)

    def cast(dst, src, name, nsplit):
        for i in range(nsplit):
            A("pool", I("dma_start", out=dst[:, i], in_=src[:, i]), writes=[("wsc", name)], lane=("cast", name), nochain=True)

    A("pool", I("dma_start", out=w_in_b[:, :, 512:1280], in_=w_in[:, :, 512:1280]), writes=[("wsc", "w_in_kv")], lane=("cast", "w_in_kv"))
    for d_, s_ in ((w1k_b, w1k), (w1v_b, w1v), (w2k_b, w2k), (w2v_b, w2v), (pek_b, pek), (pev_b, pev)):
        A("pool", I("dma_start", out=d_, in_=s_), writes=[("wsc", "cmp")], lane=("cast", "cmp"), nochain=True)
    A("pool", I("dma_start", out=w_in_b[:, :, 0:512], in_=w_in[:, :, 0:512]), writes=[("wsc", "w_in")], lane=("cast", "w_in"), nochain=True)
    A("pool", I("dma_start", out=w_in_b[:, :, 1280:2328], in_=w_in[:, :, 1280:2328]), writes=[("wsc", "w_in")], lane=("cast", "w_in"), nochain=True)
    cast(wmrg_b, wmrg, "wmrg", 8)
    cast(w_o_b, w_o, "w_o", 8)
    cast(w_up_b, w_up, "w_up", 22)
    cast(w_dn_b, w_dn, "w_dn", 2)
    cast(w_pg_b, w_pg, "w_pg", 8)
    cast(w_pp_b, w_pp, "w_pp", 2)

    def wload(src, name, kc=None):
        t, k = wbuf.next()
        tot = 1
        for d_ in src.shape[1:]:
            tot *= d_
        v = t[:, 0:tot]
        if len(src.shape) == 3:
            v = v.rearrange("p (k c) -> p k c", k=src.shape[1])
        A("sp", I("dma_start", out=v, in_=src), reads=[("wsc", name)], writes=[k], lane=("wl",) + k)
        return v, k

    xs = rot("xs", 3, [128, 1024], F32)
    grot = rot("gbc", 2, [128, 1024], F32)

    def gain(src):
        gt_, gk_ = grot.next()
        A("sp", I("dma_start", out=gt_[:], in_=src.to_broadcast([128, 1024])), writes=[gk_], lane=("gbc",) + gk_)
        return gt_, gk_

    gmx, gmxk = gain(g_mix)
    xn = rot("xn", 2, [128, 1024], BF16)
    st1 = rot("st1", 4, [128, 1], F32)
    uTb = rot("uTb", 1, [128, 8, 512], BF16)
    f5 = rot("f5", 6, [128, 512], F32)
    b5 = rot("b5", 3, [128, 512], BF16)
    cosK = T("cosK", [128, 512], F32)
    sinK = T("sinK", [128, 512], F32)
    posi = T("posi", [128, 512], I32)
    angf = T("angf", [128, 512], F32)
    kki = posi
    kkf = T("kkf", [128, 512], F32)
    xl = 0

    def fma(eng, out, in0, sc, acc, rkeys, wkey):
        assert eng == "dve"
        A("dve", I("scalar_tensor_tensor", out=out, in0=in0, scalar=sc, in1=acc, op0=ALU.mult, op1=ALU.add), reads=rkeys, writes=[wkey])

    def norm_many(srcs, gain, gkey, dst, dkey):
        prev = None
        for j_, src_ in enumerate(srcs):
            cur = norm_A(src_, gain, gkey)
            if prev is not None:
                norm_B(prev[0], prev[1], dst, dkey, (j_ - 1) * 128)
            prev = cur
        norm_B(prev[0], prev[1], dst, dkey, (len(srcs) - 1) * 128)

    def norm_A(src_rows, gain, gkey):
        nonlocal xl
        if isinstance(src_rows, tuple):
            xt, xk = src_rows
        else:
            xtile, xk = xs.next()
            xt = xtile[:]
            xl += 1
            A("sp", I("dma_start", out=xt, in_=src_rows), writes=[xk], lane=("xs",) + xk)
        s1, s1k = st1.next()
        xnt, xnk = xn.next()
        A("act", I("activation", out=xnt[:], in_=xt, func=AF.Square, accum_out=s1[:]), reads=[xk], writes=[xnk, s1k])
        A("dve", I("tensor_scalar", out=s1[:], in0=s1[:], scalar1=1.0 / 1024, scalar2=EPS, op0=ALU.mult, op1=ALU.add), reads=[s1k], writes=[s1k])
        A("act", I("activation", out=s1[:], in_=s1[:], func=AF.Sqrt), reads=[s1k], writes=[s1k])
        A("dve", I("reciprocal", out=s1[:], in_=s1[:]), reads=[s1k], writes=[s1k])
        A("dve", I("scalar_tensor_tensor", out=xnt[:], in0=xt, scalar=s1[:, 0:1], in1=gain[:], op0=ALU.mult, op1=ALU.mult), reads=[xk, s1k, gkey], writes=[xnk])
        return xnt, xnk

    def norm_B(xnt, xnk, dst, dkey, col0):
        p, pk = pT.next()
        for k in range(8):
            A("pe", I("transpose", p[:, k * 128:(k + 1) * 128], xnt[:, k * 128:(k + 1) * 128], idt[:]), reads=[xnk, "idt"], writes=[pk])
        A("act", I("activation", out=dst[:, :, col0:col0 + 128], in_=p[:].rearrange("p (k c) -> p k c", k=8), func=AF.Copy), reads=[pk], writes=[dkey])

    def rope_tables(pos_ap, n, scale):
        A("sp", I("dma_start", out=posi[:, 0:n], in_=pos_ap.to_broadcast([128, n])), writes=["posi"], lane="posi")
        A("dve", I("tensor_copy", out=angf[:, 0:n], in_=posi[:, 0:n]), reads=["posi"], writes=["angf"])
        A("dve", I("tensor_scalar", out=angf[:, 0:n], in0=angf[:, 0:n], scalar1=rpc[:, 0:1], scalar2=None, op0=ALU.mult), reads=["angf", "rpc"], writes=["angf"])
        for which, dstt, dk in ((0, sinK, "sinK"), (1, cosK, "cosK")):
            if which == 1:
                A("dve", I("tensor_scalar", out=angf[:, 0:n], in0=angf[:, 0:n], scalar1=float(np.pi / 2), scalar2=None, op0=ALU.add), reads=["angf"], writes=["angf"])
            A("dve", I("tensor_scalar", out=kkf[:, 0:n], in0=angf[:, 0:n], scalar1=float(1 / TWO_PI), scalar2=None, op0=ALU.mult), reads=["angf"], writes=["kkf"])
            A("dve", I("tensor_copy", out=kki[:, 0:n], in_=kkf[:, 0:n]), reads=["kkf"], writes=["posi"])
            A("dve", I("tensor_copy", out=kkf[:, 0:n], in_=kki[:, 0:n]), reads=["posi"], writes=["kkf"])
            A("dve", I("scalar_tensor_tensor", out=kkf[:, 0:n], in0=kkf[:, 0:n], scalar=-TWO_PI, in1=angf[:, 0:n], op0=ALU.mult, op1=ALU.add), reads=["kkf", "angf"], writes=["kkf"])
            A("dve", I("tensor_scalar", out=kkf[:, 0:n], in0=kkf[:, 0:n], scalar1=3.14159, scalar2=-3.14159, op0=ALU.min, op1=ALU.max), reads=["kkf"], writes=["kkf"])
            A("act", I("activation", out=dstt[:, 0:n], in_=kkf[:, 0:n], func=AF.Sin), reads=["kkf"], writes=[dk])
            if which == 0:
                A("dve", I("tensor_scalar", out=sinK[:, 0:n], in0=sinK[:, 0:n], scalar1=rpc[:, 1:2], scalar2=float(scale), op0=ALU.mult, op1=ALU.mult), reads=["sinK", "rpc"], writes=["sinK"])
            elif scale != 1.0:
                A("dve", I("tensor_scalar", out=cosK[:, 0:n], in0=cosK[:, 0:n], scalar1=float(scale), scalar2=None, op0=ALU.mult), reads=["cosK"], writes=["cosK"])

    def rope_apply(ps, psk, n, dsts):
        kb, kbk = b5.next()
        A("act", I("activation", out=kb[:, 0:n], in_=ps[:, 0:n], func=AF.Copy), reads=[psk], writes=[kbk])
        pr, prk = pP.next()
        A("pe", I("matmul", pr[:, 0:n], lhsT=pmb[:], rhs=kb[:, 0:n], start=True, stop=True), reads=[kbk, "pmb"], writes=[prk])
        t1, t1k = f5.next()
        t2, t2k = f5.next()
        A("dve", I("tensor_tensor", out=t1[:, 0:n], in0=ps[:, 0:n], in1=cosK[:, 0:n], op=ALU.mult), reads=[psk, "cosK"], writes=[t1k])
        A("dve", I("tensor_tensor", out=t2[:, 0:n], in0=pr[:, 0:n], in1=sinK[:, 0:n], op=ALU.mult), reads=[prk, "sinK"], writes=[t2k])
        for o_ap, okey, sl in dsts:
            A("dve", I("tensor_tensor", out=o_ap, in0=t1[sl, 0:n], in1=t2[sl, 0:n], op=ALU.add), reads=[t1k, t2k], writes=[okey])

    def pipeline(n_, s1, s2, look=2, tick=None):
        pend = []
        for i_ in range(n_):
            pend.append((i_,) + tuple(s1(i_)))
            if tick is not None:
                tick()
            if len(pend) > look:
                s2(*pend.pop(0))
        while pend:
            s2(*pend.pop(0))

    def proj_fm(w3, wk, c0, m, rhsT, rkey, n, out_ps=None, out_rows=None):
        if out_ps is None:
            p, pk = pP.next()
            o = p[0:m, 0:n]
        else:
            p, pk = out_ps
            o = p[out_rows, 0:n]
        kc = w3.shape[1]
        for k in range(kc):
            A("pe", I("matmul", o, lhsT=w3[:, k, c0:c0 + m], rhs=rhsT[:, k, 0:n], start=(k == 0), stop=(k == kc - 1)), reads=[wk, rkey], writes=[pk])
        return p, pk

    w1t = T("w1t", [128, 32, 128], BF16)
    yvF = T("yvF", [128, 4 * 514], F32)
    uTp1 = Rot([uTb.tiles[0], yvF[:, 0:2048].bitcast(BF16).rearrange("p (k c) -> p k c", k=8)], "uTp1", keys=[("uTb", 0), "yvall"])
    if stop <= 0.1:
        return finish_build()
    for sti in range(8):
        uT, uk = uTp1.next()
        norm_many([xc[(sti * 4 + j) * 128:(sti * 4 + j + 1) * 128, :] for j in range(4)], gmx, gmxk, uT, uk)
        if stop <= 0.2:
            return finish_build()
        rope_tables(posr[0:1, sti * 512:(sti + 1) * 512], 512, 1.0)
        if stop <= 0.3:
            return finish_build()
        cs = slice(sti * 512, (sti + 1) * 512)
        wv, wk = wload(w_in_b[:, :, 512:1024], "w_in_kv")
        wv2, wk2 = wload(w_in_b[:, :, 1024:1280], "w_in_kv")
        p, pk = proj_fm(wv, wk, 0, 128, uT, uk, 512)
        A("act", I("activation", out=kcT[:, cs], in_=p[:], func=AF.Copy), reads=[pk], writes=["kcT"])
        p, pk = proj_fm(wv, wk, 128, 128, uT, uk, 512)
        A("act", I("activation", out=vcT[:, cs], in_=p[:], func=AF.Copy), reads=[pk], writes=["vcT"])
        if stop <= 0.4:
            return finish_build()
        p, pk = proj_fm(wv, wk, 256, 128, uT, uk, 512)
        rope_apply(p, pk, 512, [(ksE[0][0:64, cs], ("ksE", 0), slice(0, 64)), (ksE[1][0:64, cs], ("ksE", 1), slice(64, 128))])
        p, pk = proj_fm(wv2, wk2, 0, 128, uT, uk, 512)
        rope_apply(p, pk, 512, [(kwT[:, cs], "kwT", slice(0, 128))])
        if stop <= 0.5:
            return finish_build()
        for j in range(4):
            ct = sti * 4 + j
            p, pk = pP.next()
            for k in range(8):
                A("pe", I("matmul", p[:, 0:128], lhsT=uT[:, k, j * 128:(j + 1) * 128], rhs=wv[:, k, 384:512], start=(k == 0), stop=(k == 7)), reads=[uk, wk], writes=[pk])
            for k in range(8):
                A("pe", I("matmul", p[:, 128:256], lhsT=uT[:, k, j * 128:(j + 1) * 128], rhs=wv2[:, k, 128:256], start=(k == 0), stop=(k == 7)), reads=[uk, wk2], writes=[pk])
            A("act", I("activation", out=Vall[:, ct, :, 0:64], in_=p[:, 0:256].rearrange("p (a d) -> p a d", a=4), func=AF.Copy), reads=[pk], writes=["Vall"])

    dump("kwT", kwT[:], "kwT")
    dump("ks0", ksE[0][0:64, :], ("ksE", 0))
    dump("ks1", ksE[1][0:64, :], ("ksE", 1))
    dump("kcT", kcT, "kcT")
    dump("Vall", VallF[:], "Vall")
    if stop <= 1:
        return finish_build()
    w2t = T("w2t", [128, 64], BF16)
    pet = T("pet", [128, 32], BF16)
    hb = T("hb", [128, 1], F32)
    gh = T("gh", [128, 256], BF16)
    rope_tables(posc[0:1, 0:255], 255, 1.0)
    for kv, (w1d, w2d, ped, srcT, skey) in enumerate(((w1k_b, w2k_b, pek_b, kcT, "kcT"), (w1v_b, w2v_b, pev_b, vcT, "vcT"))):
        for hh in range(2):
            A("sp", I("dma_start", out=w1t[hh * 64:(hh + 1) * 64, :, :], in_=w1d), reads=[("wsc", "cmp")], writes=["w1t"], lane="w1t")
            A("sp", I("dma_start", out=pet[hh * 64:(hh + 1) * 64, :], in_=ped), reads=[("wsc", "cmp")], writes=["pet"], lane="pet")
        A("sp", I("dma_start", out=w2t[:], in_=w2d), reads=[("wsc", "cmp")], writes=["w2t"], lane="w2t")
        if kv == 0:
            kps, kpsk = pAcc.next()
        for g in range(2):
            rs = slice(g * 64, (g + 1) * 64)
            hp, hpk = pP.next()
            for l in range(32):
                A("pe", I("matmul", hp[:, 0:255], lhsT=w1t[rs, l, :], rhs=srcT[rs, l:l + 16 * 254 + 1:16], start=(l == 0), stop=(l == 31)), reads=["w1t", skey], writes=[hpk])
            for l in range(32):
                A("pe", I("matmul", hp[:, 255:256], lhsT=w1t[rs, l, :], rhs=pet[rs, l:l + 1], start=(l == 0), stop=(l == 31)), reads=["w1t", "pet"], writes=[hpk])
            A("dve", I("tensor_copy", out=hb[:], in_=hp[:, 255:256]), reads=[hpk], writes=["hb"])
            A("act", I("activation", out=gh[:, 0:255], in_=hp[:, 0:255], func=AF.Gelu_apprx_tanh, bias=hb[:, 0:1]), reads=[hpk, "hb"], writes=["gh"])
            if kv == 0:
                A("pe", I("matmul", kps[rs, 0:255], lhsT=w2t[:], rhs=gh[:, 0:255], start=True, stop=True), reads=["w2t", "gh"], writes=[kpsk])
            else:
                for nt in range(2):
                    nn = 128 if nt == 0 else 127
                    vp, vpk = pP.next()
                    A("pe", I("matmul", vp[0:nn, 0:64], lhsT=gh[:, nt * 128:nt * 128 + nn], rhs=w2t[:], start=True, stop=True), reads=["w2t", "gh"], writes=[vpk])
                    A("act", I("activation", out=Vc[0:nn, g, nt, 0:64], in_=vp[0:nn, 0:64], func=AF.Copy), reads=[vpk], writes=["Vc"])
        if kv == 0:
            rope_apply(kps, kpsk, 255, [(kcmpT[:, 0:255], "kcmpT", slice(0, 128))])

    dump("kcmpT", kcmpT[:], "kcmpT")
    dump("Vc", Vc[:].rearrange("p a b c -> p (a b c)"), "Vc")
    if stop <= 2:
        return finish_build()
    gluT = T("gluT", [128, 4, 544], BF16)
    yv = yvF[:].rearrange("p (j c) -> p j c", j=4)
    dg = T("dg", [128, 31, 128], BF16)
    ysq = T("ysq", [128, 512], F32)
    cvT = T("cvT", [128, 4, 512], BF16)
    attnT = T("attnT", [128, 4, 512], BF16)
    attq = T("attq", [128, 512], BF16)
    mT = w1t[:].rearrange("p l h -> p (l h)").rearrange("p (k c) -> p k c", k=8)
    QW = T("QW", [128, 4, 512], BF16)
    QS = [T("QS%d" % g, [128, 4, 128], BF16) for g in range(2)]
    gts = T("gts", [128, 4, 24], F32)
    tabt = rot("tabt", 2, [128, 3, 64], F32)
    wmt = rot("wmt", 2, [128, 5, 128], BF16)
    cmt = rot("cmt", 2, [128, 2, 128], BF16)
    PTb = rot("PTb", 4, [128, 4, 128], BF16)
    osum2 = [T("osum%d" % g_, [128, 4, 64], F32) for g_ in range(2)]
    sm = rot("sm", 6, [128, 64], F32)
    rc4 = rot("rc4", 6, [128, 4], F32)
    m8 = rot("m8", 4, [128, 8], F32)
    negm2 = [T("negm%d" % g_, [128, 128], BF16) for g_ in range(2)]
    A("dve", I("memset", gluT[:], 0.0), writes=["gluT"])
    for g_ in range(2):
        A("dve", I("memset", negm2[g_][:], 0.0), writes=[("negm", g_)])

    groups = [(14, 2)] + [(16 + 4 * i, 4) for i in range(4)]
    pre_norm = None
    for gi_, (ct0, nt_) in enumerate(groups):
        n = nt_ * 128
        if pre_norm is None:
            uT, uk = uTb.next()
            norm_many([xc[(ct0 + j) * 128:(ct0 + j + 1) * 128, :] for j in range(nt_)], gmx, gmxk, uT, uk)
        else:
            uT, uk = pre_norm
        rope_tables(posr[0:1, ct0 * 128:ct0 * 128 + n], n, 0.125)
        wv, wk = wload(w_in_b[:, :, 0:512], "w_in")
        for h in range(4):
            pq = pP.next()
            proj_fm(wv, wk, h * 128, 128, uT, uk, n, out_ps=pq, out_rows=slice(0, 128))
            rope_apply(pq[0], pq[1], n, [(QW[:, h, 0:n], "QW", slice(0, 128))])
        wvg, wkg = wload(w_in_b[:, :, 1280:1304], "w_in")
        for j in range(nt_):
            p, pk = pP.next()
            for k in range(8):
                A("pe", I("matmul", p[:, 0:24], lhsT=uT[:, k, j * 128:(j + 1) * 128], rhs=wvg[:, k, 0:24], start=(k == 0), stop=(k == 7)), reads=[uk, wkg], writes=[pk])
            A("act", I("activation", out=gts[:, j, :], in_=p[:, 0:24], func=AF.Sigmoid), reads=[pk], writes=["gts"])
        for half in range(2):
            wva, wka = wload(w_in_b[:, :, 1304 + half * 256:1304 + half * 256 + 256], "w_in")
            wvb, wkb = wload(w_in_b[:, :, 1816 + half * 256:1816 + half * 256 + 256], "w_in")
            for jj in range(2):
                j = half * 2 + jj
                pa, pak = proj_fm(wva, wka, jj * 128, 128, uT, uk, n)
                pb, pbk = proj_fm(wvb, wkb, jj * 128, 128, uT, uk, n)
                sb, sbk = f5.next()
                A("act", I("activation", out=sb[:, 0:n], in_=pb[:, 0:n], func=AF.Sigmoid), reads=[pbk], writes=[sbk])
                A("dve", I("tensor_tensor", out=gluT[:, j, 32:32 + n], in0=pa[:, 0:n], in1=sb[:, 0:n], op=ALU.mult), reads=[pak, sbk], writes=["gluT"])
        def conv_steps(j, per_tick=2):
            A("pool", I("tensor_tensor", out=dg[:], in0=idt[:].unsqueeze(1).to_broadcast([128, 31, 128]), in1=cdwt[:, j, 0:31].unsqueeze(2).to_broadcast([128, 31, 128]), op=ALU.mult), reads=["idt", "cdwt"], writes=["dg"])
            yield
            cps, cpsk = conv_bank[0] if conv_bank[0] is not None else pAcc.next()
            for k in range(31):
                A("pe", I("matmul", cps[:, 0:n], lhsT=dg[:, k, :], rhs=gluT[:, j, 2 + k:2 + k + n], start=(k == 0), stop=(k == 30)), reads=["dg", "gluT"], writes=[cpsk])
                if k % per_tick == per_tick - 1:
                    yield
            A("act", I("activation", out=yv[:, j, 0:n], in_=cps[:, 0:n], func=AF.Identity, bias=cdwt[:, j, 31:32]), reads=[cpsk, "cdwt"], writes=[("yv", j), "yvall"])

        def conv_chunk(j):
            for _ in conv_steps(j):
                pass

        conv_pending = []

        def conv_tick():
            while conv_pending:
                try:
                    next(conv_pending[0])
                    return
                except StopIteration:
                    conv_pending.pop(0)

        per_tile = 4 // nt_
        deferred = []
        for j in range(nt_):
            ctq = ct0 + j
            qi = ctq - 15
            if qi < 0:
                for cj in range(j * per_tile, (j + 1) * per_tile):
                    conv_chunk(cj)
                continue
            qs = slice(j * 128, (j + 1) * 128)
            for cj in range(j * per_tile, (j + 1) * per_tile):
                conv_pending.append(conv_steps(cj))
            conv_tick()
            tb, tbk = tabt.next()
            wm, wmk = wmt.next()
            cm, cmk = cmt.next()
            A("sp", I("dma_start", out=tb[:], in_=tabs[qi]), writes=[tbk], lane=("tab",) + tbk)
            A("sp", I("dma_start", out=wm[:], in_=wmask[qi]), writes=[wmk], lane=("wm",) + wmk)
            A("sp", I("dma_start", out=cm[:], in_=cmaskd[qi]), writes=[cmk], lane=("cm",) + cmk)
            conv_bank[0] = ACC[4]

            def att_g(g):
                osum, osumk = osum2[g], ("osum", g)
                negm, negmk = negm2[g], ("negm", g)
                rs = slice(g * 64, (g + 1) * 64)
                A("dve", I("tensor_copy", out=QS[g][0:64, :, :], in_=QW[rs, :, qs]), reads=["QW"], writes=[("QS", g)])
                accc, acck = ACC[0 if g == 0 else 3]
                acco, accok = ACC[1 if g == 0 else 4]
                accw, accwk = ACC[2 if g == 0 else 1]

                def c_s1(nt):
                    sp_, spk = pP.next()
                    A("pe", I("matmul", sp_[:].rearrange("p (h q) -> p h q", h=4), lhsT=kcmpT[rs, nt * 128:(nt + 1) * 128], rhs=QW[rs, :, qs], start=True, stop=True), reads=["kcmpT", "QW"], writes=[spk])
                    pt, ptk = PTb.next()
                    A("act", I("activation", out=pt[:].rearrange("p h q -> p (h q)"), in_=sp_[:], func=AF.Exp), reads=[spk], writes=[ptk])
                    A("pool", I("tensor_tensor", out=pt[:], in0=pt[:], in1=cm[:, nt, :].unsqueeze(1).to_broadcast([128, 4, 128]), op=ALU.mult), reads=[ptk, cmk], writes=[ptk])
                    return pt, ptk

                def c_s2(nt, pt, ptk):
                    for h in range(4):
                        A("pe", I("matmul", accc[:, h * 65:(h + 1) * 65], lhsT=pt[:, h, :], rhs=ovt[:, nt, :], start=(nt == 0 and h == 0), stop=(nt == 1), skip_group_check=True), reads=[ptk, "ovt"], writes=[acck])
                    for h in range(4):
                        A("pe", I("matmul", acco[:, h * 65:(h + 1) * 65], lhsT=pt[:, h, :], rhs=Vc[:, g, nt, :], start=(nt == 0 and h == 0), stop=(nt == 1), skip_group_check=True), reads=[ptk, "Vc"], writes=[accok])

                def w_s1(kt):
                    kti = ctq - 4 + kt
                    sp_, spk = pP.next()
                    A("pe", I("matmul", sp_[:].rearrange("p (h q) -> p h q", h=4), lhsT=kwT[rs, kti * 128:(kti + 1) * 128], rhs=QW[rs, :, qs], start=True, stop=True), reads=["kwT", "QW"], writes=[spk])
                    pt, ptk = PTb.next()
                    A("act", I("activation", out=pt[:].rearrange("p h q -> p (h q)"), in_=sp_[:], func=AF.Exp), reads=[spk], writes=[ptk])
                    if kt in (0, 4) or qi + kt < 5:
                        A("pool", I("tensor_tensor", out=pt[:], in0=pt[:], in1=wm[:, kt, :].unsqueeze(1).to_broadcast([128, 4, 128]), op=ALU.mult), reads=[ptk, wmk], writes=[ptk])
                    return pt, ptk

                def w_s2(kt, pt, ptk):
                    kti = ctq - 4 + kt
                    for h in range(4):
                        A("pe", I("matmul", accw[:, h * 65:(h + 1) * 65], lhsT=pt[:, h, :], rhs=Vall[:, kti, 2 + g, :], start=(kt == 0 and h == 0), stop=(kt == 4), skip_group_check=True), reads=[ptk, "Vall"], writes=[accwk])

                pipeline(7, lambda i_: c_s1(i_) if i_ < 2 else w_s1(i_ - 2), lambda i_, pt, ptk: c_s2(i_, pt, ptk) if i_ < 2 else w_s2(i_ - 2, pt, ptk), look=1)
                while deferred:
                    deferred.pop(0)()
                rc, rck = rc4.next()
                c3 = accc[:, 0:260].rearrange("p (h c) -> p h c", h=4)
                A("dve", I("tensor_scalar", out=rc[:], in0=c3[:, :, 64], scalar1=1e-30, scalar2=None, op0=ALU.add), reads=[acck], writes=[rck])
                A("dve", I("reciprocal", out=rc[:], in_=rc[:]), reads=[rck], writes=[rck])
                c3o = acco[:, 0:260].rearrange("p (h c) -> p h c", h=4)

                def finish(acc3, acck_, br, first, rcin=None, rcink=None):
                    if rcin is None:
                        r, rk = rc4.next()
                        A("dve", I("tensor_scalar", out=r[:], in0=acc3[:, :, 64], scalar1=1e-30, scalar2=None, op0=ALU.add), reads=[acck_], writes=[rk])
                        A("dve", I("reciprocal", out=r[:], in_=r[:]), reads=[rk], writes=[rk])
                    else:
                        r, rk = rcin, rcink
                    fc, fck = rc4.next()
                    A("dve", I("tensor_tensor", out=fc[:], in0=r[:], in1=gts[:, j, br * 8 + g * 4:br * 8 + g * 4 + 4], op=ALU.mult), reads=[rk, "gts"], writes=[fck])
                    for h in range(4):
                        if first:
                            A("dve", I("tensor_scalar", out=osum[:, h, :], in0=acc3[:, h, 0:64], scalar1=fc[:, h:h + 1], scalar2=None, op0=ALU.mult), reads=[acck_, fck], writes=[osumk])
                        else:
                            A("dve", I("scalar_tensor_tensor", out=osum[:, h, :], in0=acc3[:, h, 0:64], scalar=fc[:, h:h + 1], in1=osum[:, h, :], op0=ALU.mult, op1=ALU.add), reads=[acck_, fck, osumk], writes=[osumk])

                finish(c3o, accok, 0, True, rc, rck)
                imp, impk = sm.next()
                A("dve", I("tensor_scalar", out=imp[:], in0=c3[:, 0, 0:64], scalar1=rc[:, 0:1], scalar2=None, op0=ALU.mult), reads=[acck, rck], writes=[impk])
                for h in range(1, 4):
                    A("dve", I("scalar_tensor_tensor", out=imp[:], in0=c3[:, h, 0:64], scalar=rc[:, h:h + 1], in1=imp[:], op0=ALU.mult, op1=ALU.add), reads=[acck, rck, impk], writes=[impk])
                A("dve", I("tensor_tensor", out=imp[:], in0=imp[:], in1=tb[:, 0, :], op=ALU.add), reads=[impk, tbk], writes=[impk])
                A("dve", I("tensor_tensor", out=imp[:], in0=imp[:], in1=tb[:, 1, :], op=ALU.max), reads=[impk, tbk], writes=[impk])
                ma, mak = m8.next()
                mb, mbk = m8.next()
                tmp, tmpk = sm.next()
                A("dve", I("max", out=ma[:], in_=imp[:]), reads=[impk], writes=[mak])
                A("dve", I("match_replace", out=tmp[:], in_to_replace=ma[:], in_values=imp[:], imm_value=-1e9), reads=[impk, mak], writes=[tmpk])
                A("dve", I("max", out=mb[:], in_=tmp[:]), reads=[tmpk], writes=[mbk])
                A("dve", I("tensor_scalar", out=tmp[:], in0=imp[:], scalar1=mb[:, 7:8], scalar2=None, op0=ALU.is_ge), reads=[impk, mbk], writes=[tmpk])
                A("dve", I("tensor_tensor", out=tmp[:], in0=tmp[:], in1=tb[:, 2, :], op=ALU.mult), reads=[tmpk, tbk], writes=[tmpk])
                A("dve", I("tensor_scalar", out=negm[:, 64:128], in0=tmp[:], scalar1=-1.0, scalar2=30000.0, op0=ALU.add, op1=ALU.mult), reads=[tmpk], writes=[negmk])
                yield
                if g == 0:
                    while conv_pending:
                        conv_tick()
                ptn, ptnk = pT.next()
                A("pe", I("transpose", ptn[:, 0:128], negm[:], idt[:]), reads=[negmk, "idt"], writes=[ptnk])
                A("dve", I("tensor_copy", out=QS[g][64:128, :, :], in_=ptn[64:128, 0:128].unsqueeze(1).to_broadcast([64, 4, 128])), reads=[ptnk], writes=[("QS", g)])
                finish(accw[:, 0:260].rearrange("p (h c) -> p h c", h=4), accwk, 2, False)
                accs, accsk = accc, acck
                def s_s1(kti):
                    sp_, spk = pP.next()
                    A("pe", I("matmul", sp_[:].rearrange("p (h q) -> p h q", h=4), lhsT=ksE[g][:, kti * 128:(kti + 1) * 128], rhs=QS[g][:, :, :], start=True, stop=True), reads=[("ksE", g), ("QS", g)], writes=[spk])
                    pt, ptk = PTb.next()
                    A("act", I("activation", out=pt[:].rearrange("p h q -> p (h q)"), in_=sp_[:], func=AF.Exp), reads=[spk], writes=[ptk])
                    if kti == ctq:
                        A("pool", I("tensor_tensor", out=pt[:], in0=pt[:], in1=wm[:, 4, :].unsqueeze(1).to_broadcast([128, 4, 128]), op=ALU.mult), reads=[ptk, wmk], writes=[ptk])
                    return pt, ptk

                def s_s2(kti, pt, ptk):
                    for h in range(4):
                        A("pe", I("matmul", accs[:, h * 65:(h + 1) * 65], lhsT=pt[:, h, :], rhs=Vall[:, kti, g, :], start=(kti == 0 and h == 0), stop=(kti == ctq), skip_group_check=True), reads=[ptk, "Vall"], writes=[accsk])

                pipeline(ctq + 1, s_s1, s_s2, look=1)
                finish(accs[:, 0:260].rearrange("p (h c) -> p h c", h=4), accsk, 1, False)
                A("act", I("activation", out=attq[:, g * 256:(g + 1) * 256], in_=osum[:].rearrange("p h d -> p (h d)"), func=AF.Copy), reads=[osumk], writes=["attq"])
            gens_ = [att_g(0), att_g(1)]
            next(gens_[0])
            next(gens_[1])
            for gen_ in gens_:
                for _ in gen_:
                    pass
            conv_bank[0] = None
            def _attn_T(qs=qs):
                pta, ptak = pT.next()
                for fc_ in range(4):
                    A("pe", I("transpose", pta[:, fc_ * 128:(fc_ + 1) * 128], attq[:, fc_ * 128:(fc_ + 1) * 128], idt[:]), reads=["attq", "idt"], writes=[ptak])
                A("act", I("activation", out=attnT[:, :, qs], in_=pta[:, 0:512].rearrange("p (k c) -> p k c", k=4), func=AF.Copy), reads=[ptak], writes=["attnT"])

            deferred.append(_attn_T)
            while conv_pending:
                conv_tick()

        while deferred:
            deferred.pop(0)()
        halo_t, halok = b5.next()
        A("dve", I("tensor_copy", out=halo_t[:, 0:128].rearrange("p (j c) -> p j c", j=4), in_=gluT[:, :, n:n + 32]), reads=["gluT"], writes=[halok])
        A("dve", I("tensor_copy", out=gluT[:, :, 0:32], in_=halo_t[:, 0:128].rearrange("p (j c) -> p j c", j=4)), reads=[halok], writes=["gluT"])
        pm_, pmk = pAcc.next()
        pq_, pqk = pAcc.next()
        for j in range(4):
            A("pe", I("matmul", pm_[:, 0:n], lhsT=onesf[:], rhs=yv[:, j, 0:n], start=(j == 0), stop=(j == 3)), reads=["onesf", ("yv", j)], writes=[pmk])
        for j in range(4):
            A("act", I("activation", out=ysq[:, 0:n], in_=yv[:, j, 0:n], func=AF.Square), reads=[("yv", j)], writes=["ysq"])
            A("pe", I("matmul", pq_[:, 0:n], lhsT=onesf[:], rhs=ysq[:, 0:n], start=(j == 0), stop=(j == 3)), reads=["onesf", "ysq"], writes=[pqk])
        mean, meank = f5.next()
        rstd, rstdk = f5.next()
        A("act", I("activation", out=mean[:, 0:n], in_=pm_[:, 0:n], func=AF.Copy), reads=[pmk], writes=[meank])
        A("dve", I("tensor_tensor", out=rstd[:, 0:n], in0=mean[:, 0:n], in1=mean[:, 0:n], op=ALU.mult), reads=[meank], writes=[rstdk])
        A("dve", I("scalar_tensor_tensor", out=rstd[:, 0:n], in0=rstd[:, 0:n], scalar=-1.0, in1=pq_[:, 0:n], op0=ALU.mult, op1=ALU.add), reads=[rstdk, pqk], writes=[rstdk])
        A("dve", I("tensor_scalar", out=rstd[:, 0:n], in0=rstd[:, 0:n], scalar1=0.0, scalar2=EPS, op0=ALU.max, op1=ALU.add), reads=[rstdk], writes=[rstdk])
        A("act", I("activation", out=rstd[:, 0:n], in_=rstd[:, 0:n], func=AF.Sqrt), reads=[rstdk], writes=[rstdk])
        A("dve", I("reciprocal", out=rstd[:, 0:n], in_=rstd[:, 0:n]), reads=[rstdk], writes=[rstdk])
        for j in range(4):
            d, dk = f5.next()
            A("dve", I("tensor_tensor", out=d[:, 0:n], in0=yv[:, j, 0:n], in1=mean[:, 0:n], op=ALU.subtract), reads=[("yv", j), meank], writes=[dk])
            A("dve", I("tensor_tensor", out=d[:, 0:n], in0=d[:, 0:n], in1=rstd[:, 0:n], op=ALU.mult), reads=[dk, rstdk], writes=[dk])
            A("act", I("activation", out=cvT[:, j, 0:n], in_=d[:, 0:n], func=AF.Silu, scale=cdwt[:, j, 32:33], bias=cdwt[:, j, 33:34]), reads=[dk, "cdwt"], writes=["cvT"])

        if ct0 == 14:
            j_lo = 1
        else:
            j_lo = 0
        c_lo = j_lo * 128
        nn_ = n - c_lo
        for dc in range(8):
            co = 0
            wm_, wmk_ = wload(wmrg_b[:, dc, :], "wmrg")
            wva_ = wm_[:, 0:512].rearrange("p (k c) -> p k c", k=4)
            wvb_ = wm_[:, 512:1024].rearrange("p (k c) -> p k c", k=4)
            wga = wm_[:, 1024:2048].rearrange("p (k c) -> p k c", k=8)
            wgb = wm_[:, 2048:3072].rearrange("p (k c) -> p k c", k=8)
            wka_ = wkb_ = wgak = wgbk = wmk_
            pya, pyak = pM.next()
            for k in range(4):
                A("pe", I("matmul", pya[:, 0:nn_], lhsT=wva_[:, k, co:co + 128], rhs=attnT[:, k, c_lo:n], start=(k == 0), stop=(k == 3)), reads=[wka_, "attnT"], writes=[pyak])
            pyb, pybk = pM.next()
            for k in range(4):
                A("pe", I("matmul", pyb[:, 0:nn_], lhsT=wvb_[:, k, co:co + 128], rhs=cvT[:, k, c_lo:n], start=(k == 0), stop=(k == 3)), reads=[wkb_, "cvT"], writes=[pybk])
            pga, pgak = pM.next()
            for k in range(8):
                A("pe", I("matmul", pga[:, 0:nn_], lhsT=wga[:, k, co:co + 128], rhs=uT[:, k, c_lo:n], start=(k == 0), stop=(k == 7)), reads=[wgak, uk], writes=[pgak])
            sga, sgak = f5.next()
            A("act", I("activation", out=sga[:, 0:nn_], in_=pga[:, 0:nn_], func=AF.Sigmoid), reads=[pgak], writes=[sgak])
            pgb, pgbk = pM.next()
            for k in range(8):
                A("pe", I("matmul", pgb[:, 0:nn_], lhsT=wgb[:, k, co:co + 128], rhs=uT[:, k, c_lo:n], start=(k == 0), stop=(k == 7)), reads=[wgbk, uk], writes=[pgbk])
            sgb, sgbk = f5.next()
            A("act", I("activation", out=sgb[:, 0:nn_], in_=pgb[:, 0:nn_], func=AF.Sigmoid), reads=[pgbk], writes=[sgbk])
            A("dve", I("tensor_tensor", out=sga[:, 0:nn_], in0=pya[:, 0:nn_], in1=sga[:, 0:nn_], op=ALU.mult), reads=[pyak, sgak], writes=[sgak])
            A("dve", I("tensor_tensor", out=sgb[:, 0:nn_], in0=pyb[:, 0:nn_], in1=sgb[:, 0:nn_], op=ALU.mult), reads=[pybk, sgbk], writes=[sgbk])
            A("pool", I("tensor_tensor", out=mT[:, dc, 0:nn_], in0=sga[:, 0:nn_], in1=sgb[:, 0:nn_], op=ALU.add), reads=[sgak, sgbk], writes=["w1t"])
        if gi_ + 1 < len(groups):
            ctn_, ntn_ = groups[gi_ + 1]
            uTn_, ukn_ = uTb.next()
            norm_many([xc[(ctn_ + j) * 128:(ctn_ + j + 1) * 128, :] for j in range(ntn_)], gmx, gmxk, uTn_, ukn_)
            pre_norm = (uTn_, ukn_)
        wo0, wo0k = wload(w_o_b[:, :, 0:512], "w_o")
        wo1, wo1k = wload(w_o_b[:, :, 512:1024], "w_o")
        for j in range(j_lo, nt_):
            ctq = ct0 + j
            qi = ctq - 15
            xt_, xk_ = xs.next()
            A("sp", I("dma_start", out=xt_[:], in_=xc[ctq * 128:(ctq + 1) * 128, :]), writes=[xk_], lane=("xs",) + xk_)
            ht, hk = xs.next()
            for hf, (wo_, wok_) in enumerate(((wo0, wo0k), (wo1, wo1k))):
                p, pk = pP.next()
                for k in range(8):
                    A("pe", I("matmul", p[:], lhsT=mT[:, k, (j - j_lo) * 128:(j - j_lo + 1) * 128], rhs=wo_[:, k, :], start=(k == 0), stop=(k == 7)), reads=["w1t", wok_], writes=[pk])
                A("dve", I("tensor_tensor", out=ht[:, hf * 512:(hf + 1) * 512], in0=p[:], in1=xt_[:, hf * 512:(hf + 1) * 512], op=ALU.add), reads=[pk, xk_], writes=[hk])
            A("sp", I("dma_start", out=h1s[qi * 128:(qi + 1) * 128, :], in_=ht[:]), reads=[hk], writes=[("h1s", qi)], lane=("h1st",) + hk)

    hbA = VallF[:, 0:8192].bitcast(F32).rearrange("p (j c) -> p j c", j=4)
    hbB0 = ksE[0][:, 0:4096].bitcast(F32).rearrange("p (j c) -> p j c", j=2)
    hbB1 = ksE[1][:, 0:4096].bitcast(F32).rearrange("p (j c) -> p j c", j=2)

    def hb_tile(sx, j):
        if sx == 0:
            return hbA[:, j, :], ("hbuf", 0, j), ["Vall"]
        return (hbB0 if j < 2 else hbB1)[:, j % 2, :], ("hbuf", 1, j), [("ksE", j // 2)]

    actT = arena1[:].rearrange("p (j c) -> p j c", j=22)
    uab = Rot([yv[:, i, :] for i in range(4)], "yv")
    halo = T("halo", [128, 44, 2], F32)
    pT2 = cvT[:, 0:2, :]
    pst = Rot([ysq[:, 0:256], ysq[:, 256:512]], "ysqh", fixed="ysq")
    pbf = Rot([attq[:, 0:256], attq[:, 256:512]], "attqh", fixed="attq")
    A("dve", I("memset", halo[:], 0.0), writes=["halo"])

    def ffn_load_norm(q0, nt_, sx):
        uT, uk = uTb.next()
        gffb, gffk = gain(g_ffn)
        srcs = []
        for j in range(nt_):
            ap_, k_, extra = hb_tile(sx, j)
            A("sp", I("dma_start", out=ap_, in_=h1s[(q0 + j) * 128:(q0 + j + 1) * 128, :]), reads=[("h1s", q0 + j)], writes=[k_] + extra, lane=("hb", sx, j))
            srcs.append((ap_, k_))
        norm_many(srcs, gffb, gffk, uT, uk)
        return uT, uk

    def ffn_up(nt_, uT, uk, is_halo):
        n = nt_ * 128
        for jc in range(22):
            wu_, wuk_ = wload(w_up_b[:, jc, :], "w_up")
            wu3 = wu_.rearrange("p (k c) -> p k c", k=8)
            wva, wka, wvg, wkg = wu3[:, :, 0:128], wuk_, wu3[:, :, 128:256], wuk_
            ys = []
            for idx, (wv_, wk_) in enumerate(((wva, wka), (wvg, wkg))):
                ch = jc + 22 * idx
                p, pk = proj_fm(wv_, wk_, 0, 128, uT, uk, n, out_ps=pM.next(), out_rows=slice(0, 128))
                ua, uak = uab.next()
                A("act", I("activation", out=ua[:, 2:2 + n], in_=p[:, 0:n], func=AF.Copy), reads=[pk], writes=[uak])
                A("pool", I("tensor_copy", out=ua[:, 0:2], in_=halo[:, ch, :]), reads=[("halo", ch)], writes=[uak])
                if is_halo:
                    A("pool", I("tensor_scalar", out=halo[:, ch, :], in0=ua[:, n:n + 2], scalar1=flg[:, 0:1], scalar2=None, op0=ALU.mult), reads=[uak, "flg"], writes=[("halo", ch)])
                    continue
                A("pool", I("tensor_copy", out=halo[:, ch, :], in_=ua[:, n:n + 2]), reads=[uak], writes=[("halo", ch)])
                ya, yak = f5.next()
                A("act", I("activation", out=ya[:, 0:n], in_=p[:, 0:n], func=AF.Identity, scale=fdwt[:, ch, 2:3], bias=fdwt[:, ch, 3:4]), reads=[pk, "fdwt"], writes=[yak])
                fma("dve", ya[:, 0:n], ua[:, 1:1 + n], fdwt[:, ch, 1:2], ya[:, 0:n], [uak, "fdwt", yak], yak)
                fma("dve", ya[:, 0:n], ua[:, 0:n], fdwt[:, ch, 0:1], ya[:, 0:n], [uak, "fdwt", yak], yak)
                ys.append((ya, yak))
            if is_halo:
                continue
            (ya, yak), (yg, ygk) = ys
            A("act", I("activation", out=yg[:, 0:n], in_=yg[:, 0:n], func=AF.Silu), reads=[ygk], writes=[ygk])
            A("pool", I("tensor_tensor", out=actT[:, jc, 0:n], in0=ya[:, 0:n], in1=yg[:, 0:n], op=ALU.mult), reads=[yak, ygk], writes=[("actT", jc), "kcT", "vcT"])

    def ffn_down(sx):
        for hf in range(2):
            pacc = [pAcc.next() for _ in range(4)]
            for part in range(6):
                j0_, j1_ = part * 4, min(22, part * 4 + 4)
                wv_, wk_ = wload(w_dn_b[:, hf, j0_:j1_, :], "w_dn")
                for j in range(4):
                    p, pk = pacc[j]
                    for jj in range(j0_, j1_):
                        A("pe", I("matmul", p[:], lhsT=actT[:, jj, j * 128:(j + 1) * 128], rhs=wv_[:, jj - j0_, :], start=(jj == 0), stop=(jj == 21)), reads=[("actT", jj), wk_], writes=[pk])
            for j in range(4):
                p, pk = pacc[j]
                ap_, k_, _ = hb_tile(sx, j)
                A("dve", I("tensor_tensor", out=ap_[:, hf * 512:(hf + 1) * 512], in0=p[:], in1=ap_[:, hf * 512:(hf + 1) * 512], op=ALU.add), reads=[pk, k_], writes=[k_])

    def ple_out(q0, sx):
        uT2, uk2 = uTb.next()
        gplb, gplk = gain(g_ple)
        norm_many([hb_tile(sx, j)[0:2] for j in range(4)], gplb, gplk, uT2, uk2)
        for j in range(4):
            tok0 = (q0 - 1 + j) * 128
            pt_, ptk_ = pst.next()
            A("sp", I("dma_start", out=pt_[:], in_=pmine[tok0:tok0 + 128, :]), writes=[ptk_], lane="pstl")
            pb_, pbk_ = pbf.next()
            A("act", I("activation", out=pb_[:], in_=pt_[:], func=AF.Copy), reads=[ptk_], writes=[pbk_])
            ptp, ptpk = pT.next()
            for k in range(2):
                A("pe", I("transpose", ptp[:, k * 128:(k + 1) * 128], pb_[:, k * 128:(k + 1) * 128], idt[:]), reads=[pbk_, "idt"], writes=[ptpk])
            A("act", I("activation", out=pT2[:, :, j * 128:(j + 1) * 128], in_=ptp[:, 0:256].rearrange("p (k c) -> p k c", k=2), func=AF.Copy), reads=[ptpk], writes=["cvT"])
        for hf in range(2):
            wpg, wpgk = wload(w_pg_b[:, :, hf * 512:(hf + 1) * 512], "w_pg")
            wpp, wppk = wload(w_pp_b[:, :, hf * 512:(hf + 1) * 512], "w_pp")
            for j in range(4):
                ap_, k_, _ = hb_tile(sx, j)
                pg_, pgk_ = pP.next()
                for k in range(8):
                    A("pe", I("matmul", pg_[:], lhsT=uT2[:, k, j * 128:(j + 1) * 128], rhs=wpg[:, k, :], start=(k == 0), stop=(k == 7)), reads=[uk2, wpgk], writes=[pgk_])
                pp_, ppk_ = pP.next()
                for k in range(2):
                    A("pe", I("matmul", pp_[:], lhsT=pT2[:, k, j * 128:(j + 1) * 128], rhs=wpp[:, k, :], start=(k == 0), stop=(k == 1)), reads=["cvT", wppk], writes=[ppk_])
                sg, sgk = f5.next()
                A("act", I("activation", out=sg[:], in_=pg_[:], func=AF.Sigmoid), reads=[pgk_], writes=[sgk])
                A("dve", I("tensor_tensor", out=sg[:], in0=pp_[:], in1=sg[:], op=ALU.mult), reads=[ppk_, sgk], writes=[sgk])
                A("dve", I("tensor_tensor", out=ap_[:, hf * 512:(hf + 1) * 512], in0=sg[:], in1=ap_[:, hf * 512:(hf + 1) * 512], op=ALU.add), reads=[sgk, k_], writes=[k_])
        gfnb, gfnk = gain(g_fin)
        for j in range(4):
            tok0 = (q0 - 1 + j) * 128
            ap_, k_, _ = hb_tile(sx, j)
            s1, s1k = st1.next()
            jk_, jkk_ = xn.next()
            A("act", I("activation", out=jk_[:], in_=ap_, func=AF.Square, accum_out=s1[:]), reads=[k_], writes=[jkk_, s1k])
            A("dve", I("tensor_scalar", out=s1[:], in0=s1[:], scalar1=1.0 / 1024, scalar2=EPS, op0=ALU.mult, op1=ALU.add), reads=[s1k], writes=[s1k])
            A("act", I("activation", out=s1[:], in_=s1[:], func=AF.Sqrt), reads=[s1k], writes=[s1k])
            A("dve", I("reciprocal", out=s1[:], in_=s1[:]), reads=[s1k], writes=[s1k])
            o_, ok_ = xs.next()
            A("dve", I("scalar_tensor_tensor", out=o_[:], in0=ap_, scalar=s1[:, 0:1], in1=gfnb[:], op0=ALU.mult, op1=ALU.mult), reads=[k_, s1k, gfnk], writes=[ok_])
            A("sp", I("dma_start", out=out[tok0:tok0 + 128, :], in_=o_[:]), reads=[ok_], lane="ost")

    uTh, ukh = ffn_load_norm(0, 1, 1)
    ffn_up(1, uTh, ukh, True)
    fg = [1 + 4 * i for i in range(4)]
    uT_, uk_ = ffn_load_norm(fg[0], 4, 0)
    ffn_up(4, uT_, uk_, False)
    for gi_, q0 in enumerate(fg):
        sx = gi_ % 2
        ffn_down(sx)
        if gi_ + 1 < len(fg):
            uT_, uk_ = ffn_load_norm(fg[gi_ + 1], 4, 1 - sx)
            ffn_up(4, uT_, uk_, False)
        ple_out(q0, sx)

    return finish_build()


def _host_tables(ty):
    cs = 0 if ty == 1 else 2048
    tabs = np.zeros((NQT, 128, 3, 64), np.float32)
    wmask = np.zeros((NQT, 128, 5, 128), np.float32)
    cmask = np.zeros((NQT, 128, 2, 128), np.float32)
    j = np.arange(64)
    i = np.arange(128)
    for qi in range(NQT):
        T_ = 15 + qi
        cq = T_ * 128 + i
        valid = (64 * j[None, :] <= cq[:, None]) & (64 * j[None, :] >= cs) & (cq[:, None] >= cs)
        cur = cq // 64
        forced = np.full((128, 64), -1000.0, np.float32)
        forced[(j[None, :] == cur[:, None] - 1)] = 1e6
        forced[(j[None, :] == cur[:, None])] = 2e6
        forced[:, cs // 64] = 3e6
        forced[~valid] = -1000.0
        tabs[qi, :, 0, :] = np.where(valid, 0.0, -100.0)
        tabs[qi, :, 1, :] = forced
        tabs[qi, :, 2, :] = valid.astype(np.float32)
        for kt in range(5):
            ck = (T_ - 4 + kt) * 128 + i
            vis = (ck[:, None] >= cs) & (ck[:, None] <= cq[None, :]) & (cq[None, :] - ck[:, None] < 512) & (cq[None, :] >= cs)
            wmask[qi, :, kt, :] = vis
        for nt in range(2):
            nblk = nt * 128 + i
            vis = (nblk[:, None] < 255) & (16 * nblk[:, None] + 31 <= cq[None, :]) & (16 * nblk[:, None] >= cs) & (cq[None, :] >= cs)
            cmask[qi, :, nt, :] = vis
    bf = ml_dtypes.bfloat16
    return tabs, wmask.astype(bf), cmask.astype(bf)


def _host_consts():
    bf = ml_dtypes.bfloat16
    ident = np.eye(128, dtype=np.float32).astype(bf)
    pm = np.zeros((64, 64), np.float32)
    for d2 in range(8):
        pm[d2 + 8, d2] = 1.0
        pm[d2, d2 + 8] = 1.0
    pmbd = np.zeros((128, 128), np.float32)
    pmbd[:64, :64] = pm
    pmbd[64:, 64:] = pm
    ropec = np.zeros((128, 2), np.float32)
    inv = (500000.0 ** (-np.arange(0, 16, 2, dtype=np.float32) / 16)).astype(np.float32)
    for p in range(128):
        d = p % 64
        ropec[p, 0] = inv[d % 8] if d < 16 else 0.0
        ropec[p, 1] = -1.0 if d < 8 else 1.0
    emat = np.zeros((64, 4096), np.float32)
    for jj in range(64):
        emat[jj, jj * 64:(jj + 1) * 64] = 1.0
    nblk = np.arange(256)
    jb = np.arange(64)
    ov = np.clip(np.minimum(nblk[:, None] * 16 + 32, jb[None, :] * 64 + 64) - np.maximum(nblk[:, None] * 16, jb[None, :] * 64), 0, None).astype(np.float32) / 16
    ov[255] = 0
    ovm = np.zeros((128, 2, 65), np.float32)
    ovm[:, 0, :64] = ov[:128]
    ovm[:, 1, :64] = ov[128:]
    ovm[:, :, 64] = 1.0
    ovm[127, 1, 64] = 0.0
    return dict(ident=ident, pmbd=pmbd.astype(bf), ropec=ropec, emat=emat.astype(bf), ovm=ovm.astype(bf))


def _kc(w, kc):
    return np.ascontiguousarray(w.reshape(kc, 128, w.shape[1]).transpose(1, 0, 2))


_DBG = ()
_STOP = 99


def kernel(**inp):
    f = lambda a: np.ascontiguousarray(np.asarray(a))
    x = f(inp["x"]); p = f(inp["p"]); positions = f(inp["positions"]).astype(np.int32)
    shared = dict(
        w_in=_kc(f(inp["w_in"])[0], 8),
        g_mix=f(inp["norm_mix_g"]).reshape(1, 1024), g_ffn=f(inp["norm_ffn_g"]).reshape(1, 1024),
        g_ple=f(inp["norm_ple_g"]).reshape(1, 1024), g_fin=f(inp["norm_final_g"]).reshape(1, 1024),
        w1k=np.ascontiguousarray(f(inp["cmp_k_w1"])[0].reshape(32, 64, 128).transpose(1, 0, 2)),
        w1v=np.ascontiguousarray(f(inp["cmp_v_w1"])[0].reshape(32, 64, 128).transpose(1, 0, 2)),
        w2k=f(inp["cmp_k_w2"])[0], w2v=f(inp["cmp_v_w2"])[0],
        pek=np.ascontiguousarray(f(inp["pe_k"])[0].T), pev=np.ascontiguousarray(f(inp["pe_v"])[0].T),
        w_o=_kc(f(inp["w_o"])[0], 8),
        w_pg=_kc(f(inp["w_ple_gate"])[0], 8), w_pp=_kc(f(inp["w_ple_proj"])[0], 2),
    )
    cd = np.concatenate([f(inp["conv_dw_w"])[0, :, 0, :], f(inp["conv_dw_b"]), f(inp["conv_ln_g"]), f(inp["conv_ln_b"])], 0)
    shared["cdw"] = np.ascontiguousarray(cd.reshape(34, 4, 128).transpose(2, 1, 0))
    fd = np.concatenate([f(inp["ffn_dw_w"])[0, :, 0, :], f(inp["ffn_dw_b"])], 0)
    shared["fdw"] = np.ascontiguousarray(fd.reshape(4, 44, 128).transpose(2, 1, 0))
    war, wbr, wir = _kc(f(inp["w_a"])[0], 4), _kc(f(inp["w_b"])[0], 4), shared["w_in"]
    wq_ = wir[:, :, 0:512].reshape(128, 8, 2, 4, 64).transpose(0, 1, 3, 2, 4).reshape(128, 8, 512)
    wir = wir.copy()
    wir[:, :, 0:512] = wq_
    shared["w_in"] = wir
    wm = np.empty((128, 8, 3072), np.float32)
    for dc in range(8):
        wm[:, dc, 0:512] = war[:, :, dc * 128:(dc + 1) * 128].reshape(128, 512)
        wm[:, dc, 512:1024] = wbr[:, :, dc * 128:(dc + 1) * 128].reshape(128, 512)
        wm[:, dc, 1024:2048] = wir[:, :, 2328 + dc * 128:2328 + (dc + 1) * 128].reshape(128, 1024)
        wm[:, dc, 2048:3072] = wir[:, :, 3352 + dc * 128:3352 + (dc + 1) * 128].reshape(128, 1024)
    shared["wmrg"] = wm
    wur = _kc(f(inp["w_up"])[0], 8)
    wu = np.empty((128, 22, 8, 256), np.float32)
    for jc in range(22):
        wu[:, jc, :, 0:128] = wur[:, :, jc * 128:(jc + 1) * 128]
        wu[:, jc, :, 128:256] = wur[:, :, 2816 + jc * 128:2816 + (jc + 1) * 128]
    shared["w_up"] = wu.reshape(128, 22, 2048)
    wdr = _kc(f(inp["w_down"])[0], 22)
    shared["w_dn"] = np.ascontiguousarray(wdr.reshape(128, 22, 2, 512).transpose(0, 2, 1, 3))
    shared.update(_host_consts())
    tables = {ty: _host_tables(ty) for ty in (0, 1)}
    in_maps = []
    for c in range(8):
        b, ty = c // 2, c % 2
        m = dict(shared)
        if ty == 1:
            m["xc"] = x[b]
            m["posr"] = positions[b].reshape(1, 4096)
        else:
            m["xc"] = np.concatenate([np.zeros((2048, 1024), np.float32), x[b, :2048]], 0)
            m["posr"] = np.concatenate([np.zeros(2048, np.int32), positions[b, :2048]]).reshape(1, 4096)
        m["posc"] = np.ascontiguousarray(m["posr"][:, 31:4096:16])
        m["pmine"] = np.ascontiguousarray(p[0, b, ty * 2048:(ty + 1) * 2048])
        m["tabs"], m["wmask"], m["cmaskd"] = tables[ty]
        m["flagd"] = np.full((128, 1), float(ty), np.float32)
        in_maps.append(m)
    nc = bass.Bass("TRN2", target_bir_lowering=False)
    build(nc, _DBG, _STOP)
    res = run_bass_kernel_spmd(nc, in_maps, core_ids=list(range(8)))
    outp = np.zeros((4, 4096, 1024), np.float32)
    for c in range(8):
        b, ty = c // 2, c % 2
        outp[b, ty * 2048:(ty + 1) * 2048] = res.results[c]["out"]
    kernel.last = res
    if _DBG:
        kernel.dbg = [{n_: res.results[c]["dbg_" + n_] for n_, _ in _DBG} for c in range(8)]
    return outp
```

```python
import numpy as np
import ml_dtypes
from contextlib import ExitStack
import concourse.bass as bass
import concourse.mybir as mybir
from concourse.bass_utils import run_bass_kernel_spmd

F32 = mybir.dt.float32
BF16 = mybir.dt.bfloat16
I32 = mybir.dt.int32
ALU = mybir.AluOpType
AF = mybir.ActivationFunctionType

NQT = 17
EPS = 1e-6
TWO_PI = float(2 * np.pi)


class Sched:
    ENGS = ("pe", "act", "dve", "pool", "sp")

    def __init__(self):
        self.ops = []
        self.lastw = {}
        self.readers = {}
        self.dma_readers = {}
        self.lane_last = {}
        self.psum_acc = {}

    max_ops = 10 ** 9

    def add(self, eng, fn, reads=(), writes=(), lane=None, nochain=False):
        idx = len(self.ops)
        if idx >= self.max_ops:
            return
        deps = set()
        for k in reads:
            for i in self.lastw.get(k, {}).values():
                deps.add(i)
        for k in writes:
            for i in self.lastw.get(k, {}).values():
                deps.add(i)
            for i in self.readers.get(k, {}).values():
                deps.add(i)
            for i in self.dma_readers.get(k, ()):
                deps.add(i)
        if lane is not None and lane in self.lane_last and not nochain:
            deps.add(self.lane_last[lane])
        pk_ = [k for k in list(reads) + list(writes) if isinstance(k, tuple) and k[0] in ("pT", "pP", "pAcc")]
        for k in pk_:
            for e2, i in self.psum_acc.setdefault(k, {}).items():
                if e2 != eng:
                    deps.add(i)
            self.psum_acc[k][eng] = idx
        pruned = set()
        for d in deps:
            p = self.ops[d]
            if p["lane"] is None and lane is None and p["eng"] == eng and eng == "pe":
                continue
            pruned.add(d)
        self.ops.append(dict(eng=eng, fn=fn, deps=sorted(pruned), lane=lane, has_dep=False))
        for d in pruned:
            self.ops[d]["has_dep"] = True
        for k in writes:
            self.lastw.setdefault(k, {})[eng if lane is None else ("dma", lane)] = idx
            self.readers[k] = {}
            self.dma_readers[k] = []
        for k in reads:
            if lane is None:
                self.readers.setdefault(k, {})[eng] = idx
            else:
                self.dma_readers.setdefault(k, []).append(idx)
        if lane is not None:
            self.lane_last[lane] = idx
        return idx

    def emit(self, nc, final_wait_lanes=()):
        ops = self.ops
        lanes = []
        for o in ops:
            if o["lane"] is not None and o["lane"] not in lanes:
                lanes.append(o["lane"])
        with ExitStack() as st:
            esem = {e: st.enter_context(nc.semaphore("s_" + e)) for e in self.ENGS}
            lsem = {l: st.enter_context(nc.semaphore("l_%d" % i)) for i, l in enumerate(lanes)}
            ecnt = {e: 0 for e in self.ENGS}
            lcnt = {l: 0 for l in lanes}
            for o in ops:
                if o["lane"] is not None:
                    lcnt[o["lane"]] += 16
                    o["sem"], o["val"], o["inc"] = lsem[o["lane"]], lcnt[o["lane"]], 16
                elif o["has_dep"]:
                    ecnt[o["eng"]] += 1
                    o["sem"], o["val"], o["inc"] = esem[o["eng"]], ecnt[o["eng"]], 1
            final_vals = {l: lcnt[l] for l in final_wait_lanes if l in lcnt}
            block = st.enter_context(nc.Block())

            def run_engine(ename, eh):
                waited = {}
                for o in ops:
                    if o["eng"] != ename:
                        continue
                    for d in o["deps"]:
                        p = ops[d]
                        s, v = p["sem"], p["val"]
                        if waited.get(id(s), 0) >= v:
                            continue
                        eh.wait_ge(s, v)
                        waited[id(s)] = v
                    ins = o["fn"](eh)
                    if o["lane"] is not None or o["has_dep"]:
                        ins.then_inc(o["sem"], o["inc"])
                if ename == "sp":
                    for l, v in final_vals.items():
                        eh.wait_ge(lsem[l], v)

            @block.tensor
            def _(e):
                run_engine("pe", e)

            @block.scalar
            def _(e):
                run_engine("act", e)

            @block.vector
            def _(e):
                run_engine("dve", e)

            @block.gpsimd
            def _(e):
                run_engine("pool", e)

            @block.sync
            def _(e):
                run_engine("sp", e)


def I(name, *args, **kw):
    return lambda e: getattr(e, name)(*args, **kw)


class Rot:
    def __init__(self, tiles, name, fixed=None, keys=None):
        self.tiles = tiles
        self.name = name
        self.fixed = fixed
        self.keys = keys
        self.i = 0

    def next(self):
        t = self.tiles[self.i % len(self.tiles)]
        k = (self.name, self.i % len(self.tiles)) if self.fixed is None else self.fixed
        if self.keys is not None:
            k = self.keys[self.i % len(self.tiles)]
        self.i += 1
        return t, k


def build(nc, dbg_names=(), stop=99):
    S = Sched()
    A = S.add
    st = ExitStack()
    DI = lambda n, s, d: nc.dram_tensor(n, s, d, kind="ExternalInput").ap()
    xc = DI("xc", [4096, 1024], F32)
    posr = DI("posr", [1, 4096], I32)
    pmine = DI("pmine", [2048, 256], F32)
    posc = DI("posc", [1, 255], I32)
    w_in = DI("w_in", [128, 8, 4376], F32)
    g_mix = DI("g_mix", [1, 1024], F32)
    g_ffn = DI("g_ffn", [1, 1024], F32)
    g_ple = DI("g_ple", [1, 1024], F32)
    g_fin = DI("g_fin", [1, 1024], F32)
    w1k = DI("w1k", [64, 32, 128], F32)
    w1v = DI("w1v", [64, 32, 128], F32)
    w2k = DI("w2k", [128, 64], F32)
    w2v = DI("w2v", [128, 64], F32)
    pek = DI("pek", [64, 32], F32)
    pev = DI("pev", [64, 32], F32)
    cdw = DI("cdw", [128, 4, 34], F32)
    wmrg = DI("wmrg", [128, 8, 3072], F32)
    w_o = DI("w_o", [128, 8, 1024], F32)
    w_up = DI("w_up", [128, 22, 2048], F32)
    fdw = DI("fdw", [128, 44, 4], F32)
    w_dn = DI("w_dn", [128, 2, 22, 512], F32)
    w_pg = DI("w_pg", [128, 8, 1024], F32)
    w_pp = DI("w_pp", [128, 2, 1024], F32)
    ident = DI("ident", [128, 128], BF16)
    pmbd = DI("pmbd", [128, 128], BF16)
    ropec = DI("ropec", [128, 2], F32)
    emat = DI("emat", [64, 4096], BF16)
    ovm = DI("ovm", [128, 2, 65], BF16)
    tabs = DI("tabs", [NQT, 128, 3, 64], F32)
    wmask = DI("wmask", [NQT, 128, 5, 128], BF16)
    cmaskd = DI("cmaskd", [NQT, 128, 2, 128], BF16)
    flagd = DI("flagd", [128, 1], F32)
    out = nc.dram_tensor("out", [2048, 1024], F32, kind="ExternalOutput").ap()
    h1s = nc.dram_tensor("h1s", [NQT * 128, 1024], F32, kind="Internal").ap()
    SCR = lambda n_, sh: nc.dram_tensor(n_, sh, BF16, kind="Internal").ap()
    w_in_b = SCR("w_in_b", [128, 8, 4376])
    wmrg_b = SCR("wmrg_b", [128, 8, 3072])
    w_o_b = SCR("w_o_b", [128, 8, 1024])
    w_up_b = SCR("w_up_b", [128, 22, 2048])
    w_dn_b = SCR("w_dn_b", [128, 2, 22, 512])
    w_pg_b = SCR("w_pg_b", [128, 8, 1024])
    w_pp_b = SCR("w_pp_b", [128, 2, 1024])
    w1k_b, w1v_b = SCR("w1k_b", [64, 32, 128]), SCR("w1v_b", [64, 32, 128])
    w2k_b, w2v_b = SCR("w2k_b", [128, 64]), SCR("w2v_b", [128, 64])
    pek_b, pev_b = SCR("pek_b", [64, 32]), SCR("pev_b", [64, 32])
    dbg = {n: nc.dram_tensor("dbg_" + n, s, F32, kind="ExternalOutput").ap() for n, s in dbg_names}

    T = lambda name, shape, dt: st.enter_context(nc.sbuf_tensor(name, shape, dt))
    PS = lambda name, shape, dt: st.enter_context(nc.psum_tensor(name, shape, dt))

    def rot(name, n, shape, dt, psum=False):
        return Rot([(PS if psum else T)("%s%d" % (name, i), shape, dt) for i in range(n)], name)

    def dump(name, ap, key):
        if name in dbg:
            A("pool", I("dma_start", out=dbg[name], in_=ap), reads=[key], lane="dbg_" + name)

    def finish_build():
        print("NOPS", len(S.ops))
        S.emit(nc, final_wait_lanes=["ost"] + ["dbg_" + n_ for n_ in dbg])
        st.close()
        return nc

    pT = rot("pT", 1, [128, 1024], BF16, psum=True)
    pP = rot("pP", 3, [128, 512], F32, psum=True)
    pAcc = rot("pAcc", 4, [128, 512], F32, psum=True)
    pM = Rot(pP.tiles + pAcc.tiles, "pM", keys=[("pP", i) for i in range(3)] + [("pAcc", i) for i in range(4)])

    idt = T("idt", [128, 128], BF16)
    pmb = T("pmb", [128, 128], BF16)
    rpc = T("rpc", [128, 2], F32)
    ovt = T("ovt", [128, 2, 65], BF16)
    flg = T("flg", [128, 1], F32)
    cdwt = T("cdwt", [128, 4, 34], F32)
    fdwt = T("fdwt", [128, 44, 4], F32)
    onesf = T("onesf", [128, 128], F32)
    halfpi = T("halfpi", [128, 1], F32)
    ld = 0

    def load(dst, src, key, eng="sp"):
        nonlocal ld
        ld += 1
        A(eng, I("dma_start", out=dst, in_=src), writes=[key], lane="ld%d" % (ld % 4) if eng == "sp" else "ldp%d" % (ld % 4))

    load(idt[:], ident, "idt")
    load(pmb[:], pmbd, "pmb")
    load(rpc[:], ropec, "rpc")
    load(ovt[:], ovm, "ovt")
    load(flg[:], flagd, "flg")
    load(cdwt[:], cdw, "cdwt")
    load(fdwt[:], fdw, "fdwt")
    A("dve", I("memset", onesf[:], 1.0 / 512), writes=["onesf"])
    A("dve", I("memset", halfpi[:], float(np.pi / 2)), writes=["halfpi"])

    ksE = [T("ksE%d" % g, [128, 4096], BF16) for g in range(2)]
    kwT = T("kwT", [128, 4096], BF16)
    VallF = T("VallF", [128, 32 * 4 * 65], BF16)
    Vall = VallF[:].rearrange("p (a b c) -> p a b c", a=32, b=4)
    arena1 = T("arena1", [128, 22 * 512], BF16)
    kcT = arena1[:, 0:4096]
    vcT = arena1[:, 4096:8192]
    kcmpT = T("kcmpT", [128, 256], BF16)
    Vc = T("Vc", [128, 2, 2, 65], BF16)
    for g in range(2):
        load(ksE[g][64:128, :], emat, ("ksE", g))
    A("dve", I("memset", Vall[:], 1.0), writes=["Vall"])
    A("dve", I("memset", Vc[:], 0.0), writes=["Vc"])
    A("dve", I("memset", Vc[:, :, :, 64:65], 1.0), writes=["Vc"])
    A("dve", I("memset", kcmpT[:], 0.0), writes=["kcmpT"])

    wbuf = rot("wbuf", 3, [128, 8 * 512], BF16)

    def cast(dst, src, name, nsplit):
        for i in range(nsplit):
            A("pool", I("dma_start", out=dst[:, i], in_=src[:, i]), writes=[("wsc", name)], lane=("cast", name), nochain=True)

    A("pool", I("dma_start", out=w_in_b[:, :, 512:1280], in_=w_in[:, :, 512:1280]), writes=[("wsc", "w_in_kv")], lane=("cast", "w_in_kv"))
    for d_, s_ in ((w1k_b, w1k), (w1v_b, w1v), (w2k_b, w2k), (w2v_b, w2v), (pek_b, pek), (pev_b, pev)):
        A("pool", I("dma_start", out=d_, in_=s_), writes=[("wsc", "cmp")], lane=("cast", "cmp"), nochain=True)
    A("pool", I("dma_start", out=w_in_b[:, :, 0:512], in_=w_in[:, :, 0:512]), writes=[("wsc", "w_in")], lane=("cast", "w_in"), nochain=True)
    A("pool", I("dma_start", out=w_in_b[:, :, 1280:2328], in_=w_in[:, :, 1280:2328]), writes=[("wsc", "w_in")], lane=("cast", "w_in"), nochain=True)
    cast(wmrg_b, wmrg, "wmrg", 8)
    cast(w_o_b, w_o, "w_o", 8)
    cast(w_up_b, w_up, "w_up", 22)
    cast(w_dn_b, w_dn, "w_dn", 2)
    cast(w_pg_b, w_pg, "w_pg", 8)
    cast(w_pp_b, w_pp, "w_pp", 2)

    def wload(src, name, kc=None):
        t, k = wbuf.next()
        tot = 1
        for d_ in src.shape[1:]:
            tot *= d_
        v = t[:, 0:tot]
        if len(src.shape) == 3:
            v = v.rearrange("p (k c) -> p k c", k=src.shape[1])
        A("sp", I("dma_start", out=v, in_=src), reads=[("wsc", name)], writes=[k], lane=("wl",) + k)
        return v, k

    xs = rot("xs", 3, [128, 1024], F32)
    grot = rot("gbc", 2, [128, 1024], F32)

    def gain(src):
        gt_, gk_ = grot.next()
        A("sp", I("dma_start", out=gt_[:], in_=src.to_broadcast([128, 1024])), writes=[gk_], lane=("gbc",) + gk_)
        return gt_, gk_

    gmx, gmxk = gain(g_mix)
    xn = rot("xn", 2, [128, 1024], BF16)
    st1 = rot("st1", 4, [128, 1], F32)
    uTb = rot("uTb", 1, [128, 8, 512], BF16)
    f5 = rot("f5", 6, [128, 512], F32)
    b5 = rot("b5", 3, [128, 512], BF16)
    cosK = T("cosK", [128, 512], F32)
    sinK = T("sinK", [128, 512], F32)
    posi = T("posi", [128, 512], I32)
    angf = T("angf", [128, 512], F32)
    kki = posi
    kkf = T("kkf", [128, 512], F32)
    xl = 0

    def fma(eng, out, in0, sc, acc, rkeys, wkey):
        assert eng == "dve"
        A("dve", I("scalar_tensor_tensor", out=out, in0=in0, scalar=sc, in1=acc, op0=ALU.mult, op1=ALU.add), reads=rkeys, writes=[wkey])

    def norm_many(srcs, gain, gkey, dst, dkey):
        prev = None
        for j_, src_ in enumerate(srcs):
            cur = norm_A(src_, gain, gkey)
            if prev is not None:
                norm_B(prev[0], prev[1], dst, dkey, (j_ - 1) * 128)
            prev = cur
        norm_B(prev[0], prev[1], dst, dkey, (len(srcs) - 1) * 128)

    def norm_A(src_rows, gain, gkey):
        nonlocal xl
        if isinstance(src_rows, tuple):
            xt, xk = src_rows
        else:
            xtile, xk = xs.next()
            xt = xtile[:]
            xl += 1
            A("sp", I("dma_start", out=xt, in_=src_rows), writes=[xk], lane=("xs",) + xk)
        s1, s1k = st1.next()
        xnt, xnk = xn.next()
        A("act", I("activation", out=xnt[:], in_=xt, func=AF.Square, accum_out=s1[:]), reads=[xk], writes=[xnk, s1k])
        A("dve", I("tensor_scalar", out=s1[:], in0=s1[:], scalar1=1.0 / 1024, scalar2=EPS, op0=ALU.mult, op1=ALU.add), reads=[s1k], writes=[s1k])
        A("act", I("activation", out=s1[:], in_=s1[:], func=AF.Sqrt), reads=[s1k], writes=[s1k])
        A("dve", I("reciprocal", out=s1[:], in_=s1[:]), reads=[s1k], writes=[s1k])
        A("dve", I("scalar_tensor_tensor", out=xnt[:], in0=xt, scalar=s1[:, 0:1], in1=gain[:], op0=ALU.mult, op1=ALU.mult), reads=[xk, s1k, gkey], writes=[xnk])
        return xnt, xnk

    def norm_B(xnt, xnk, dst, dkey, col0):
        p, pk = pT.next()
        for k in range(8):
            A("pe", I("transpose", p[:, k * 128:(k + 1) * 128], xnt[:, k * 128:(k + 1) * 128], idt[:]), reads=[xnk, "idt"], writes=[pk])
        A("act", I("activation", out=dst[:, :, col0:col0 + 128], in_=p[:].rearrange("p (k c) -> p k c", k=8), func=AF.Copy), reads=[pk], writes=[dkey])

    def rope_tables(pos_ap, n, scale):
        A("sp", I("dma_start", out=posi[:, 0:n], in_=pos_ap.to_broadcast([128, n])), writes=["posi"], lane="posi")
        A("dve", I("tensor_copy", out=angf[:, 0:n], in_=posi[:, 0:n]), reads=["posi"], writes=["angf"])
        A("dve", I("tensor_scalar", out=angf[:, 0:n], in0=angf[:, 0:n], scalar1=rpc[:, 0:1], scalar2=None, op0=ALU.mult), reads=["angf", "rpc"], writes=["angf"])
        for which, dstt, dk in ((0, sinK, "sinK"), (1, cosK, "cosK")):
            if which == 1:
                A("dve", I("tensor_scalar", out=angf[:, 0:n], in0=angf[:, 0:n], scalar1=float(np.pi / 2), scalar2=None, op0=ALU.add), reads=["angf"], writes=["angf"])
            A("dve", I("tensor_scalar", out=kkf[:, 0:n], in0=angf[:, 0:n], scalar1=float(1 / TWO_PI), scalar2=None, op0=ALU.mult), reads=["angf"], writes=["kkf"])
            A("dve", I("tensor_copy", out=kki[:, 0:n], in_=kkf[:, 0:n]), reads=["kkf"], writes=["posi"])
            A("dve", I("tensor_copy", out=kkf[:, 0:n], in_=kki[:, 0:n]), reads=["posi"], writes=["kkf"])
            A("dve", I("scalar_tensor_tensor", out=kkf[:, 0:n], in0=kkf[:, 0:n], scalar=-TWO_PI, in1=angf[:, 0:n], op0=ALU.mult, op1=ALU.add), reads=["kkf", "angf"], writes=["kkf"])
            A("dve", I("tensor_scalar", out=kkf[:, 0:n], in0=kkf[:, 0:n], scalar1=3.14159, scalar2=-3.14159, op0=ALU.min, op1=ALU.max), reads=["kkf"], writes=["kkf"])
            A("act", I("activation", out=dstt[:, 0:n], in_=kkf[:, 0:n], func=AF.Sin), reads=["kkf"], writes=[dk])
            if which == 0:
                A("dve", I("tensor_scalar", out=sinK[:, 0:n], in0=sinK[:, 0:n], scalar1=rpc[:, 1:2], scalar2=float(scale), op0=ALU.mult, op1=ALU.mult), reads=["sinK", "rpc"], writes=["sinK"])
            elif scale != 1.0:
                A("dve", I("tensor_scalar", out=cosK[:, 0:n], in0=cosK[:, 0:n], scalar1=float(scale), scalar2=None, op0=ALU.mult), reads=["cosK"], writes=["cosK"])

    def rope_apply(ps, psk, n, dsts):
        kb, kbk = b5.next()
        A("act", I("activation", out=kb[:, 0:n], in_=ps[:, 0:n], func=AF.Copy), reads=[psk], writes=[kbk])
        pr, prk = pP.next()
        A("pe", I("matmul", pr[:, 0:n], lhsT=pmb[:], rhs=kb[:, 0:n], start=True, stop=True), reads=[kbk, "pmb"], writes=[prk])
        t1, t1k = f5.next()
        t2, t2k = f5.next()
        A("dve", I("tensor_tensor", out=t1[:, 0:n], in0=ps[:, 0:n], in1=cosK[:, 0:n], op=ALU.mult), reads=[psk, "cosK"], writes=[t1k])
        A("dve", I("tensor_tensor", out=t2[:, 0:n], in0=pr[:, 0:n], in1=sinK[:, 0:n], op=ALU.mult), reads=[prk, "sinK"], writes=[t2k])
        for o_ap, okey, sl in dsts:
            A("dve", I("tensor_tensor", out=o_ap, in0=t1[sl, 0:n], in1=t2[sl, 0:n], op=ALU.add), reads=[t1k, t2k], writes=[okey])

    def pipeline(n_, s1, s2, look=2, tick=None):
        pend = []
        for i_ in range(n_):
            pend.append((i_,) + tuple(s1(i_)))
            if tick is not None:
                tick()
            if len(pend) > look:
                s2(*pend.pop(0))
        while pend:
            s2(*pend.pop(0))

    def proj_fm(w3, wk, c0, m, rhsT, rkey, n, out_ps=None, out_rows=None):
        if out_ps is None:
            p, pk = pP.next()
            o = p[0:m, 0:n]
        else:
            p, pk = out_ps
            o = p[out_rows, 0:n]
        kc = w3.shape[1]
        for k in range(kc):
            A("pe", I("matmul", o, lhsT=w3[:, k, c0:c0 + m], rhs=rhsT[:, k, 0:n], start=(k == 0), stop=(k == kc - 1)), reads=[wk, rkey], writes=[pk])
        return p, pk

    w1t = T("w1t", [128, 32, 128], BF16)
    yvF = T("yvF", [128, 4 * 514], F32)
    uTp1 = Rot([uTb.tiles[0], yvF[:, 0:2048].bitcast(BF16).rearrange("p (k c) -> p k c", k=8)], "uTp1", keys=[("uTb", 0), "yvall"])
    if stop <= 0.1:
        return finish_build()
    for sti in range(8):
        uT, uk = uTp1.next()
        norm_many([xc[(sti * 4 + j) * 128:(sti * 4 + j + 1) * 128, :] for j in range(4)], gmx, gmxk, uT, uk)
        if stop <= 0.2:
            return finish_build()
        rope_tables(posr[0:1, sti * 512:(sti + 1) * 512], 512, 1.0)
        if stop <= 0.3:
            return finish_build()
        cs = slice(sti * 512, (sti + 1) * 512)
        wv, wk = wload(w_in_b[:, :, 512:1024], "w_in_kv")
        wv2, wk2 = wload(w_in_b[:, :, 1024:1280], "w_in_kv")
        p, pk = proj_fm(wv, wk, 0, 128, uT, uk, 512)
        A("act", I("activation", out=kcT[:, cs], in_=p[:], func=AF.Copy), reads=[pk], writes=["kcT"])
        p, pk = proj_fm(wv, wk, 128, 128, uT, uk, 512)
        A("act", I("activation", out=vcT[:, cs], in_=p[:], func=AF.Copy), reads=[pk], writes=["vcT"])
        if stop <= 0.4:
            return finish_build()
        p, pk = proj_fm(wv, wk, 256, 128, uT, uk, 512)
        rope_apply(p, pk, 512, [(ksE[0][0:64, cs], ("ksE", 0), slice(0, 64)), (ksE[1][0:64, cs], ("ksE", 1), slice(64, 128))])
        p, pk = proj_fm(wv2, wk2, 0, 128, uT, uk, 512)
        rope_apply(p, pk, 512, [(kwT[:, cs], "kwT", slice(0, 128))])
        if stop <= 0.5:
            return finish_build()
        for j in range(4):
            ct = sti * 4 + j
            p, pk = pP.next()
            for k in range(8):
                A("pe", I("matmul", p[:, 0:128], lhsT=uT[:, k, j * 128:(j + 1) * 128], rhs=wv[:, k, 384:512], start=(k == 0), stop=(k == 7)), reads=[uk, wk], writes=[pk])
            for k in range(8):
                A("pe", I("matmul", p[:, 128:256], lhsT=uT[:, k, j * 128:(j + 1) * 128], rhs=wv2[:, k, 128:256], start=(k == 0), stop=(k == 7)), reads=[uk, wk2], writes=[pk])
            A("act", I("activation", out=Vall[:, ct, :, 0:64], in_=p[:, 0:256].rearrange("p (a d) -> p a d", a=4), func=AF.Copy), reads=[pk], writes=["Vall"])

    dump("kwT", kwT[:], "kwT")
    dump("ks0", ksE[0][0:64, :], ("ksE", 0))
    dump("ks1", ksE[1][0:64, :], ("ksE", 1))
    dump("kcT", kcT, "kcT")
    dump("Vall", VallF[:], "Vall")
    if stop <= 1:
        return finish_build()
    w2t = T("w2t", [128, 64], BF16)
    pet = T("pet", [128, 32], BF16)
    hb = T("hb", [128, 1], F32)
    gh = T("gh", [128, 256], BF16)
    rope_tables(posc[0:1, 0:255], 255, 1.0)
    for kv, (w1d, w2d, ped, srcT, skey) in enumerate(((w1k_b, w2k_b, pek_b, kcT, "kcT"), (w1v_b, w2v_b, pev_b, vcT, "vcT"))):
        for hh in range(2):
            A("sp", I("dma_start", out=w1t[hh * 64:(hh + 1) * 64, :, :], in_=w1d), reads=[("wsc", "cmp")], writes=["w1t"], lane="w1t")
            A("sp", I("dma_start", out=pet[hh * 64:(hh + 1) * 64, :], in_=ped), reads=[("wsc", "cmp")], writes=["pet"], lane="pet")
        A("sp", I("dma_start", out=w2t[:], in_=w2d), reads=[("wsc", "cmp")], writes=["w2t"], lane="w2t")
        if kv == 0:
            kps, kpsk = pAcc.next()
        for g in range(2):
            rs = slice(g * 64, (g + 1) * 64)
            hp, hpk = pP.next()
            for l in range(32):
                A("pe", I("matmul", hp[:, 0:255], lhsT=w1t[rs, l, :], rhs=srcT[rs, l:l + 16 * 254 + 1:16], start=(l == 0), stop=(l == 31)), reads=["w1t", skey], writes=[hpk])
            for l in range(32):
                A("pe", I("matmul", hp[:, 255:256], lhsT=w1t[rs, l, :], rhs=pet[rs, l:l + 1], start=(l == 0), stop=(l == 31)), reads=["w1t", "pet"], writes=[hpk])
            A("dve", I("tensor_copy", out=hb[:], in_=hp[:, 255:256]), reads=[hpk], writes=["hb"])
            A("act", I("activation", out=gh[:, 0:255], in_=hp[:, 0:255], func=AF.Gelu_apprx_tanh, bias=hb[:, 0:1]), reads=[hpk, "hb"], writes=["gh"])
            if kv == 0:
                A("pe", I("matmul", kps[rs, 0:255], lhsT=w2t[:], rhs=gh[:, 0:255], start=True, stop=True), reads=["w2t", "gh"], writes=[kpsk])
            else:
                for nt in range(2):
                    nn = 128 if nt == 0 else 127
                    vp, vpk = pP.next()
                    A("pe", I("matmul", vp[0:nn, 0:64], lhsT=gh[:, nt * 128:nt * 128 + nn], rhs=w2t[:], start=True, stop=True), reads=["w2t", "gh"], writes=[vpk])
                    A("act", I("activation", out=Vc[0:nn, g, nt, 0:64], in_=vp[0:nn, 0:64], func=AF.Copy), reads=[vpk], writes=["Vc"])
        if kv == 0:
            rope_apply(kps, kpsk, 255, [(kcmpT[:, 0:255], "kcmpT", slice(0, 128))])

    dump("kcmpT", kcmpT[:], "kcmpT")
    dump("Vc", Vc[:].rearrange("p a b c -> p (a b c)"), "Vc")
    if stop <= 2:
        return finish_build()
    gluT = T("gluT", [128, 4, 544], BF16)
    yv = yvF[:].rearrange("p (j c) -> p j c", j=4)
    dg = T("dg", [128, 31, 128], BF16)
    ysq = T("ysq", [128, 512], F32)
    cvT = T("cvT", [128, 4, 512], BF16)
    attnT = T("attnT", [128, 4, 512], BF16)
    attq = T("attq", [128, 512], BF16)
    mT = w1t[:].rearrange("p l h -> p (l h)").rearrange("p (k c) -> p k c", k=8)
    QW = T("QW", [128, 4, 512], BF16)
    QS = [T("QS%d" % g, [128, 4, 128], BF16) for g in range(2)]
    gts = T("gts", [128, 4, 24], F32)
    tabt = rot("tabt", 2, [128, 3, 64], F32)
    wmt = rot("wmt", 2, [128, 5, 128], BF16)
    cmt = rot("cmt", 2, [128, 2, 128], BF16)
    PTb = rot("PTb", 4, [128, 4, 128], BF16)
    osum = T("osum", [128, 4, 64], F32)
    sm = rot("sm", 6, [128, 64], F32)
    rc4 = rot("rc4", 6, [128, 4], F32)
    m8 = rot("m8", 4, [128, 8], F32)
    negm = T("negm", [128, 128], BF16)
    A("dve", I("memset", gluT[:], 0.0), writes=["gluT"])
    A("dve", I("memset", negm[:], 0.0), writes=["negm"])

    groups = [(14, 2)] + [(16 + 4 * i, 4) for i in range(4)]
    pre_norm = None
    wo_pending = []
    for gi_, (ct0, nt_) in enumerate(groups):
        n = nt_ * 128
        if pre_norm is None:
            uT, uk = uTb.next()
            norm_many([xc[(ct0 + j) * 128:(ct0 + j + 1) * 128, :] for j in range(nt_)], gmx, gmxk, uT, uk)
        else:
            uT, uk = pre_norm
        rope_tables(posr[0:1, ct0 * 128:ct0 * 128 + n], n, 0.125)
        wv, wk = wload(w_in_b[:, :, 0:512], "w_in")
        for h in range(4):
            pq = pP.next()
            proj_fm(wv, wk, h * 128, 128, uT, uk, n, out_ps=pq, out_rows=slice(0, 128))
            rope_apply(pq[0], pq[1], n, [(QW[:, h, 0:n], "QW", slice(0, 128))])
        wvg, wkg = wload(w_in_b[:, :, 1280:1304], "w_in")
        for j in range(nt_):
            p, pk = pP.next()
            for k in range(8):
                A("pe", I("matmul", p[:, 0:24], lhsT=uT[:, k, j * 128:(j + 1) * 128], rhs=wvg[:, k, 0:24], start=(k == 0), stop=(k == 7)), reads=[uk, wkg], writes=[pk])
            A("act", I("activation", out=gts[:, j, :], in_=p[:, 0:24], func=AF.Sigmoid), reads=[pk], writes=["gts"])
        for half in range(2):
            wva, wka = wload(w_in_b[:, :, 1304 + half * 256:1304 + half * 256 + 256], "w_in")
            wvb, wkb = wload(w_in_b[:, :, 1816 + half * 256:1816 + half * 256 + 256], "w_in")
            for jj in range(2):
                j = half * 2 + jj
                pa, pak = proj_fm(wva, wka, jj * 128, 128, uT, uk, n)
                pb, pbk = proj_fm(wvb, wkb, jj * 128, 128, uT, uk, n)
                sb, sbk = f5.next()
                A("act", I("activation", out=sb[:, 0:n], in_=pb[:, 0:n], func=AF.Sigmoid), reads=[pbk], writes=[sbk])
                A("dve", I("tensor_tensor", out=gluT[:, j, 32:32 + n], in0=pa[:, 0:n], in1=sb[:, 0:n], op=ALU.mult), reads=[pak, sbk], writes=["gluT"])
        def conv_steps(j, per_tick=2):
            A("pool", I("tensor_tensor", out=dg[:], in0=idt[:].unsqueeze(1).to_broadcast([128, 31, 128]), in1=cdwt[:, j, 0:31].unsqueeze(2).to_broadcast([128, 31, 128]), op=ALU.mult), reads=["idt", "cdwt"], writes=["dg"])
            yield
            cps, cpsk = pAcc.next()
            for k in range(31):
                A("pe", I("matmul", cps[:, 0:n], lhsT=dg[:, k, :], rhs=gluT[:, j, 2 + k:2 + k + n], start=(k == 0), stop=(k == 30)), reads=["dg", "gluT"], writes=[cpsk])
                if k % per_tick == per_tick - 1:
                    yield
            A("act", I("activation", out=yv[:, j, 0:n], in_=cps[:, 0:n], func=AF.Identity, bias=cdwt[:, j, 31:32]), reads=[cpsk, "cdwt"], writes=[("yv", j), "yvall"])

        def conv_chunk(j):
            for _ in conv_steps(j):
                pass

        conv_pending = []

        def conv_tick():
            while conv_pending:
                try:
                    next(conv_pending[0])
                    return
                except StopIteration:
                    conv_pending.pop(0)

        per_tile = 4 // nt_
        deferred = []
        for j in range(nt_):
            ctq = ct0 + j
            qi = ctq - 15
            if qi < 0:
                for cj in range(j * per_tile, (j + 1) * per_tile):
                    conv_chunk(cj)
                continue
            qs = slice(j * 128, (j + 1) * 128)
            for cj in range(j * per_tile, (j + 1) * per_tile):
                conv_pending.append(conv_steps(cj))
            conv_tick()
            tb, tbk = tabt.next()
            wm, wmk = wmt.next()
            cm, cmk = cmt.next()
            A("sp", I("dma_start", out=tb[:], in_=tabs[qi]), writes=[tbk], lane=("tab",) + tbk)
            A("sp", I("dma_start", out=wm[:], in_=wmask[qi]), writes=[wmk], lane=("wm",) + wmk)
            A("sp", I("dma_start", out=cm[:], in_=cmaskd[qi]), writes=[cmk], lane=("cm",) + cmk)
            for g in range(2):
                rs = slice(g * 64, (g + 1) * 64)
                A("dve", I("tensor_copy", out=QS[g][0:64, :, :], in_=QW[rs, :, qs]), reads=["QW"], writes=[("QS", g)])
                accc, acck = pAcc.next()
                acco, accok = pAcc.next()
                accw, accwk = pAcc.next()

                def c_s1(nt):
                    sp_, spk = pP.next()
                    A("pe", I("matmul", sp_[:].rearrange("p (h q) -> p h q", h=4), lhsT=kcmpT[rs, nt * 128:(nt + 1) * 128], rhs=QW[rs, :, qs], start=True, stop=True), reads=["kcmpT", "QW"], writes=[spk])
                    pt, ptk = PTb.next()
                    A("act", I("activation", out=pt[:].rearrange("p h q -> p (h q)"), in_=sp_[:], func=AF.Exp), reads=[spk], writes=[ptk])
                    A("pool", I("tensor_tensor", out=pt[:], in0=pt[:], in1=cm[:, nt, :].unsqueeze(1).to_broadcast([128, 4, 128]), op=ALU.mult), reads=[ptk, cmk], writes=[ptk])
                    return pt, ptk

                def c_s2(nt, pt, ptk):
                    for h in range(4):
                        A("pe", I("matmul", accc[:, h * 65:(h + 1) * 65], lhsT=pt[:, h, :], rhs=ovt[:, nt, :], start=(nt == 0 and h == 0), stop=(nt == 1), skip_group_check=True), reads=[ptk, "ovt"], writes=[acck])
                    for h in range(4):
                        A("pe", I("matmul", acco[:, h * 65:(h + 1) * 65], lhsT=pt[:, h, :], rhs=Vc[:, g, nt, :], start=(nt == 0 and h == 0), stop=(nt == 1), skip_group_check=True), reads=[ptk, "Vc"], writes=[accok])

                def w_s1(kt):
                    kti = ctq - 4 + kt
                    sp_, spk = pP.next()
                    A("pe", I("matmul", sp_[:].rearrange("p (h q) -> p h q", h=4), lhsT=kwT[rs, kti * 128:(kti + 1) * 128], rhs=QW[rs, :, qs], start=True, stop=True), reads=["kwT", "QW"], writes=[spk])
                    pt, ptk = PTb.next()
                    A("act", I("activation", out=pt[:].rearrange("p h q -> p (h q)"), in_=sp_[:], func=AF.Exp), reads=[spk], writes=[ptk])
                    if kt in (0, 4) or qi + kt < 5:
                        A("pool", I("tensor_tensor", out=pt[:], in0=pt[:], in1=wm[:, kt, :].unsqueeze(1).to_broadcast([128, 4, 128]), op=ALU.mult), reads=[ptk, wmk], writes=[ptk])
                    return pt, ptk

                def w_s2(kt, pt, ptk):
                    kti = ctq - 4 + kt
                    for h in range(4):
                        A("pe", I("matmul", accw[:, h * 65:(h + 1) * 65], lhsT=pt[:, h, :], rhs=Vall[:, kti, 2 + g, :], start=(kt == 0 and h == 0), stop=(kt == 4), skip_group_check=True), reads=[ptk, "Vall"], writes=[accwk])

                pipeline(7, lambda i_: c_s1(i_) if i_ < 2 else w_s1(i_ - 2), lambda i_, pt, ptk: c_s2(i_, pt, ptk) if i_ < 2 else w_s2(i_ - 2, pt, ptk))
                while deferred:
                    deferred.pop(0)()
                rc, rck = rc4.next()
                c3 = accc[:, 0:260].rearrange("p (h c) -> p h c", h=4)
                A("dve", I("tensor_scalar", out=rc[:], in0=c3[:, :, 64], scalar1=1e-30, scalar2=None, op0=ALU.add), reads=[acck], writes=[rck])
                A("dve", I("reciprocal", out=rc[:], in_=rc[:]), reads=[rck], writes=[rck])
                imp, impk = sm.next()
                A("dve", I("tensor_scalar", out=imp[:], in0=c3[:, 0, 0:64], scalar1=rc[:, 0:1], scalar2=None, op0=ALU.mult), reads=[acck, rck], writes=[impk])
                for h in range(1, 4):
                    A("dve", I("scalar_tensor_tensor", out=imp[:], in0=c3[:, h, 0:64], scalar=rc[:, h:h + 1], in1=imp[:], op0=ALU.mult, op1=ALU.add), reads=[acck, rck, impk], writes=[impk])
                A("dve", I("tensor_tensor", out=imp[:], in0=imp[:], in1=tb[:, 0, :], op=ALU.add), reads=[impk, tbk], writes=[impk])
                A("dve", I("tensor_tensor", out=imp[:], in0=imp[:], in1=tb[:, 1, :], op=ALU.max), reads=[impk, tbk], writes=[impk])
                ma, mak = m8.next()
                mb, mbk = m8.next()
                tmp, tmpk = sm.next()
                A("dve", I("max", out=ma[:], in_=imp[:]), reads=[impk], writes=[mak])
                A("dve", I("match_replace", out=tmp[:], in_to_replace=ma[:], in_values=imp[:], imm_value=-1e9), reads=[impk, mak], writes=[tmpk])
                A("dve", I("max", out=mb[:], in_=tmp[:]), reads=[tmpk], writes=[mbk])
                A("dve", I("tensor_scalar", out=tmp[:], in0=imp[:], scalar1=mb[:, 7:8], scalar2=None, op0=ALU.is_ge), reads=[impk, mbk], writes=[tmpk])
                A("dve", I("tensor_tensor", out=tmp[:], in0=tmp[:], in1=tb[:, 2, :], op=ALU.mult), reads=[tmpk, tbk], writes=[tmpk])
                A("dve", I("tensor_scalar", out=negm[:, 64:128], in0=tmp[:], scalar1=-1.0, scalar2=30000.0, op0=ALU.add, op1=ALU.mult), reads=[tmpk], writes=["negm"])
                c3o = acco[:, 0:260].rearrange("p (h c) -> p h c", h=4)

                def finish(acc3, acck_, br, first, rcin=None, rcink=None):
                    if rcin is None:
                        r, rk = rc4.next()
                        A("dve", I("tensor_scalar", out=r[:], in0=acc3[:, :, 64], scalar1=1e-30, scalar2=None, op0=ALU.add), reads=[acck_], writes=[rk])
                        A("dve", I("reciprocal", out=r[:], in_=r[:]), reads=[rk], writes=[rk])
                    else:
                        r, rk = rcin, rcink
                    fc, fck = rc4.next()
                    A("dve", I("tensor_tensor", out=fc[:], in0=r[:], in1=gts[:, j, br * 8 + g * 4:br * 8 + g * 4 + 4], op=ALU.mult), reads=[rk, "gts"], writes=[fck])
                    for h in range(4):
                        if first:
                            A("dve", I("tensor_scalar", out=osum[:, h, :], in0=acc3[:, h, 0:64], scalar1=fc[:, h:h + 1], scalar2=None, op0=ALU.mult), reads=[acck_, fck], writes=["osum"])
                        else:
                            A("dve", I("scalar_tensor_tensor", out=osum[:, h, :], in0=acc3[:, h, 0:64], scalar=fc[:, h:h + 1], in1=osum[:, h, :], op0=ALU.mult, op1=ALU.add), reads=[acck_, fck, "osum"], writes=["osum"])

                finish(c3o, accok, 0, True, rc, rck)
                if g == 0 and wo_pending:
                    f_, a_, b_ = wo_pending.pop(0)
                    f_(a_, b_)
                if g == 1:
                    while conv_pending:
                        conv_tick()
                ptn, ptnk = pT.next()
                A("pe", I("transpose", ptn[:, 0:128], negm[:], idt[:]), reads=["negm", "idt"], writes=[ptnk])
                A("dve", I("tensor_copy", out=QS[g][64:128, :, :], in_=ptn[64:128, 0:128].unsqueeze(1).to_broadcast([64, 4, 128])), reads=[ptnk], writes=[("QS", g)])
                finish(accw[:, 0:260].rearrange("p (h c) -> p h c", h=4), accwk, 2, False)
                accs, accsk = pAcc.next()
                def s_s1(kti):
                    sp_, spk = pP.next()
                    A("pe", I("matmul", sp_[:].rearrange("p (h q) -> p h q", h=4), lhsT=ksE[g][:, kti * 128:(kti + 1) * 128], rhs=QS[g][:, :, :], start=True, stop=True), reads=[("ksE", g), ("QS", g)], writes=[spk])
                    pt, ptk = PTb.next()
                    A("act", I("activation", out=pt[:].rearrange("p h q -> p (h q)"), in_=sp_[:], func=AF.Exp), reads=[spk], writes=[ptk])
                    if kti == ctq:
                        A("pool", I("tensor_tensor", out=pt[:], in0=pt[:], in1=wm[:, 4, :].unsqueeze(1).to_broadcast([128, 4, 128]), op=ALU.mult), reads=[ptk, wmk], writes=[ptk])
                    return pt, ptk

                def s_s2(kti, pt, ptk):
                    for h in range(4):
                        A("pe", I("matmul", accs[:, h * 65:(h + 1) * 65], lhsT=pt[:, h, :], rhs=Vall[:, kti, g, :], start=(kti == 0 and h == 0), stop=(kti == ctq), skip_group_check=True), reads=[ptk, "Vall"], writes=[accsk])

                pipeline(ctq + 1, s_s1, s_s2)
                finish(accs[:, 0:260].rearrange("p (h c) -> p h c", h=4), accsk, 1, False)
                A("act", I("activation", out=attq[:, g * 256:(g + 1) * 256], in_=osum[:].rearrange("p h d -> p (h d)"), func=AF.Copy), reads=["osum"], writes=["attq"])
            def _attn_T(qs=qs):
                pta, ptak = pT.next()
                for fc_ in range(4):
                    A("pe", I("transpose", pta[:, fc_ * 128:(fc_ + 1) * 128], attq[:, fc_ * 128:(fc_ + 1) * 128], idt[:]), reads=["attq", "idt"], writes=[ptak])
                A("act", I("activation", out=attnT[:, :, qs], in_=pta[:, 0:512].rearrange("p (k c) -> p k c", k=4), func=AF.Copy), reads=[ptak], writes=["attnT"])

            deferred.append(_attn_T)
            while conv_pending:
                conv_tick()

        while deferred:
            deferred.pop(0)()
        halo_t, halok = b5.next()
        A("dve", I("tensor_copy", out=halo_t[:, 0:128].rearrange("p (j c) -> p j c", j=4), in_=gluT[:, :, n:n + 32]), reads=["gluT"], writes=[halok])
        A("dve", I("tensor_copy", out=gluT[:, :, 0:32], in_=halo_t[:, 0:128].rearrange("p (j c) -> p j c", j=4)), reads=[halok], writes=["gluT"])
        pm_, pmk = pAcc.next()
        pq_, pqk = pAcc.next()
        for j in range(4):
            A("pe", I("matmul", pm_[:, 0:n], lhsT=onesf[:], rhs=yv[:, j, 0:n], start=(j == 0), stop=(j == 3)), reads=["onesf", ("yv", j)], writes=[pmk])
        for j in range(4):
            A("act", I("activation", out=ysq[:, 0:n], in_=yv[:, j, 0:n], func=AF.Square), reads=[("yv", j)], writes=["ysq"])
            A("pe", I("matmul", pq_[:, 0:n], lhsT=onesf[:], rhs=ysq[:, 0:n], start=(j == 0), stop=(j == 3)), reads=["onesf", "ysq"], writes=[pqk])
        mean, meank = f5.next()
        rstd, rstdk = f5.next()
        A("act", I("activation", out=mean[:, 0:n], in_=pm_[:, 0:n], func=AF.Copy), reads=[pmk], writes=[meank])
        A("dve", I("tensor_tensor", out=rstd[:, 0:n], in0=mean[:, 0:n], in1=mean[:, 0:n], op=ALU.mult), reads=[meank], writes=[rstdk])
        A("dve", I("scalar_tensor_tensor", out=rstd[:, 0:n], in0=rstd[:, 0:n], scalar=-1.0, in1=pq_[:, 0:n], op0=ALU.mult, op1=ALU.add), reads=[rstdk, pqk], writes=[rstdk])
        A("dve", I("tensor_scalar", out=rstd[:, 0:n], in0=rstd[:, 0:n], scalar1=0.0, scalar2=EPS, op0=ALU.max, op1=ALU.add), reads=[rstdk], writes=[rstdk])
        A("act", I("activation", out=rstd[:, 0:n], in_=rstd[:, 0:n], func=AF.Sqrt), reads=[rstdk], writes=[rstdk])
        A("dve", I("reciprocal", out=rstd[:, 0:n], in_=rstd[:, 0:n]), reads=[rstdk], writes=[rstdk])
        for j in range(4):
            d, dk = f5.next()
            A("dve", I("tensor_tensor", out=d[:, 0:n], in0=yv[:, j, 0:n], in1=mean[:, 0:n], op=ALU.subtract), reads=[("yv", j), meank], writes=[dk])
            A("dve", I("tensor_tensor", out=d[:, 0:n], in0=d[:, 0:n], in1=rstd[:, 0:n], op=ALU.mult), reads=[dk, rstdk], writes=[dk])
            A("act", I("activation", out=cvT[:, j, 0:n], in_=d[:, 0:n], func=AF.Silu, scale=cdwt[:, j, 32:33], bias=cdwt[:, j, 33:34]), reads=[dk, "cdwt"], writes=["cvT"])

        if ct0 == 14:
            j_lo = 1
        else:
            j_lo = 0
        c_lo = j_lo * 128
        nn_ = n - c_lo
        while wo_pending:
            f_, a_, b_ = wo_pending.pop(0)
            f_(a_, b_)
        for dc in range(8):
            co = 0
            wm_, wmk_ = wload(wmrg_b[:, dc, :], "wmrg")
            wva_ = wm_[:, 0:512].rearrange("p (k c) -> p k c", k=4)
            wvb_ = wm_[:, 512:1024].rearrange("p (k c) -> p k c", k=4)
            wga = wm_[:, 1024:2048].rearrange("p (k c) -> p k c", k=8)
            wgb = wm_[:, 2048:3072].rearrange("p (k c) -> p k c", k=8)
            wka_ = wkb_ = wgak = wgbk = wmk_
            pya, pyak = pM.next()
            for k in range(4):
                A("pe", I("matmul", pya[:, 0:nn_], lhsT=wva_[:, k, co:co + 128], rhs=attnT[:, k, c_lo:n], start=(k == 0), stop=(k == 3)), reads=[wka_, "attnT"], writes=[pyak])
            pyb, pybk = pM.next()
            for k in range(4):
                A("pe", I("matmul", pyb[:, 0:nn_], lhsT=wvb_[:, k, co:co + 128], rhs=cvT[:, k, c_lo:n], start=(k == 0), stop=(k == 3)), reads=[wkb_, "cvT"], writes=[pybk])
            pga, pgak = pM.next()
            for k in range(8):
                A("pe", I("matmul", pga[:, 0:nn_], lhsT=wga[:, k, co:co + 128], rhs=uT[:, k, c_lo:n], start=(k == 0), stop=(k == 7)), reads=[wgak, uk], writes=[pgak])
            sga, sgak = f5.next()
            A("act", I("activation", out=sga[:, 0:nn_], in_=pga[:, 0:nn_], func=AF.Sigmoid), reads=[pgak], writes=[sgak])
            pgb, pgbk = pM.next()
            for k in range(8):
                A("pe", I("matmul", pgb[:, 0:nn_], lhsT=wgb[:, k, co:co + 128], rhs=uT[:, k, c_lo:n], start=(k == 0), stop=(k == 7)), reads=[wgbk, uk], writes=[pgbk])
            sgb, sgbk = f5.next()
            A("act", I("activation", out=sgb[:, 0:nn_], in_=pgb[:, 0:nn_], func=AF.Sigmoid), reads=[pgbk], writes=[sgbk])
            A("dve", I("tensor_tensor", out=sga[:, 0:nn_], in0=pya[:, 0:nn_], in1=sga[:, 0:nn_], op=ALU.mult), reads=[pyak, sgak], writes=[sgak])
            A("dve", I("tensor_tensor", out=sgb[:, 0:nn_], in0=pyb[:, 0:nn_], in1=sgb[:, 0:nn_], op=ALU.mult), reads=[pybk, sgbk], writes=[sgbk])
            A("pool", I("tensor_tensor", out=mT[:, dc, 0:nn_], in0=sga[:, 0:nn_], in1=sgb[:, 0:nn_], op=ALU.add), reads=[sgak, sgbk], writes=["w1t"])
        if gi_ + 1 < len(groups):
            ctn_, ntn_ = groups[gi_ + 1]
            uTn_, ukn_ = uTb.next()
            norm_many([xc[(ctn_ + j) * 128:(ctn_ + j + 1) * 128, :] for j in range(ntn_)], gmx, gmxk, uTn_, ukn_)
            pre_norm = (uTn_, ukn_)
        def wo_tile(ctq, jrel):
            qi = ctq - 15
            xt_, xk_ = xs.next()
            A("sp", I("dma_start", out=xt_[:], in_=xc[ctq * 128:(ctq + 1) * 128, :]), writes=[xk_], lane=("xs",) + xk_)
            ht, hk = xs.next()
            for hf in range(2):
                wo_, wok_ = wload(w_o_b[:, :, hf * 512:(hf + 1) * 512], "w_o")
                p, pk = pP.next()
                for k in range(8):
                    A("pe", I("matmul", p[:], lhsT=mT[:, k, jrel * 128:(jrel + 1) * 128], rhs=wo_[:, k, :], start=(k == 0), stop=(k == 7)), reads=["w1t", wok_], writes=[pk])
                A("dve", I("tensor_tensor", out=ht[:, hf * 512:(hf + 1) * 512], in0=p[:], in1=xt_[:, hf * 512:(hf + 1) * 512], op=ALU.add), reads=[pk, xk_], writes=[hk])
            A("sp", I("dma_start", out=h1s[qi * 128:(qi + 1) * 128, :], in_=ht[:]), reads=[hk], writes=[("h1s", qi)], lane=("h1st",) + hk)

        for j in range(j_lo, nt_):
            wo_pending.append((wo_tile, ct0 + j, j - j_lo))
        if gi_ + 1 == len(groups):
            while wo_pending:
                f_, a_, b_ = wo_pending.pop(0)
                f_(a_, b_)

    hbA = VallF[:, 0:8192].bitcast(F32).rearrange("p (j c) -> p j c", j=4)
    hbB0 = ksE[0][:, 0:4096].bitcast(F32).rearrange("p (j c) -> p j c", j=2)
    hbB1 = ksE[1][:, 0:4096].bitcast(F32).rearrange("p (j c) -> p j c", j=2)

    def hb_tile(sx, j):
        if sx == 0:
            return hbA[:, j, :], ("hbuf", 0, j), ["Vall"]
        return (hbB0 if j < 2 else hbB1)[:, j % 2, :], ("hbuf", 1, j), [("ksE", j // 2)]

    actT = arena1[:].rearrange("p (j c) -> p j c", j=22)
    uab = Rot([yv[:, i, :] for i in range(4)], "yv")
    halo = T("halo", [128, 44, 2], F32)
    pT2 = cvT[:, 0:2, :]
    pst = Rot([ysq[:, 0:256], ysq[:, 256:512]], "ysqh", fixed="ysq")
    pbf = Rot([attq[:, 0:256], attq[:, 256:512]], "attqh", fixed="attq")
    A("dve", I("memset", halo[:], 0.0), writes=["halo"])

    def ffn_load_norm(q0, nt_, sx):
        uT, uk = uTb.next()
        gffb, gffk = gain(g_ffn)
        srcs = []
        for j in range(nt_):
            ap_, k_, extra = hb_tile(sx, j)
            A("sp", I("dma_start", out=ap_, in_=h1s[(q0 + j) * 128:(q0 + j + 1) * 128, :]), reads=[("h1s", q0 + j)], writes=[k_] + extra, lane=("hb", sx, j))
            srcs.append((ap_, k_))
        norm_many(srcs, gffb, gffk, uT, uk)
        return uT, uk

    def ffn_up(nt_, uT, uk, is_halo):
        n = nt_ * 128
        for jc in range(22):
            wu_, wuk_ = wload(w_up_b[:, jc, :], "w_up")
            wu3 = wu_.rearrange("p (k c) -> p k c", k=8)
            wva, wka, wvg, wkg = wu3[:, :, 0:128], wuk_, wu3[:, :, 128:256], wuk_
            ys = []
            for idx, (wv_, wk_) in enumerate(((wva, wka), (wvg, wkg))):
                ch = jc + 22 * idx
                p, pk = proj_fm(wv_, wk_, 0, 128, uT, uk, n, out_ps=pM.next(), out_rows=slice(0, 128))
                ua, uak = uab.next()
                A("act", I("activation", out=ua[:, 2:2 + n], in_=p[:, 0:n], func=AF.Copy), reads=[pk], writes=[uak])
                A("pool", I("tensor_copy", out=ua[:, 0:2], in_=halo[:, ch, :]), reads=[("halo", ch)], writes=[uak])
                if is_halo:
                    A("pool", I("tensor_scalar", out=halo[:, ch, :], in0=ua[:, n:n + 2], scalar1=flg[:, 0:1], scalar2=None, op0=ALU.mult), reads=[uak, "flg"], writes=[("halo", ch)])
                    continue
                A("pool", I("tensor_copy", out=halo[:, ch, :], in_=ua[:, n:n + 2]), reads=[uak], writes=[("halo", ch)])
                ya, yak = f5.next()
                A("act", I("activation", out=ya[:, 0:n], in_=p[:, 0:n], func=AF.Identity, scale=fdwt[:, ch, 2:3], bias=fdwt[:, ch, 3:4]), reads=[pk, "fdwt"], writes=[yak])
                fma("dve", ya[:, 0:n], ua[:, 1:1 + n], fdwt[:, ch, 1:2], ya[:, 0:n], [uak, "fdwt", yak], yak)
                fma("dve", ya[:, 0:n], ua[:, 0:n], fdwt[:, ch, 0:1], ya[:, 0:n], [uak, "fdwt", yak], yak)
                ys.append((ya, yak))
            if is_halo:
                continue
            (ya, yak), (yg, ygk) = ys
            A("act", I("activation", out=yg[:, 0:n], in_=yg[:, 0:n], func=AF.Silu), reads=[ygk], writes=[ygk])
            A("pool", I("tensor_tensor", out=actT[:, jc, 0:n], in0=ya[:, 0:n], in1=yg[:, 0:n], op=ALU.mult), reads=[yak, ygk], writes=[("actT", jc), "kcT", "vcT"])

    def ffn_down(sx):
        for hf in range(2):
            pacc = [pAcc.next() for _ in range(4)]
            for part in range(6):
                j0_, j1_ = part * 4, min(22, part * 4 + 4)
                wv_, wk_ = wload(w_dn_b[:, hf, j0_:j1_, :], "w_dn")
                for j in range(4):
                    p, pk = pacc[j]
                    for jj in range(j0_, j1_):
                        A("pe", I("matmul", p[:], lhsT=actT[:, jj, j * 128:(j + 1) * 128], rhs=wv_[:, jj - j0_, :], start=(jj == 0), stop=(jj == 21)), reads=[("actT", jj), wk_], writes=[pk])
            for j in range(4):
                p, pk = pacc[j]
                ap_, k_, _ = hb_tile(sx, j)
                A("dve", I("tensor_tensor", out=ap_[:, hf * 512:(hf + 1) * 512], in0=p[:], in1=ap_[:, hf * 512:(hf + 1) * 512], op=ALU.add), reads=[pk, k_], writes=[k_])

    def ple_out(q0, sx):
        uT2, uk2 = uTb.next()
        gplb, gplk = gain(g_ple)
        norm_many([hb_tile(sx, j)[0:2] for j in range(4)], gplb, gplk, uT2, uk2)
        for j in range(4):
            tok0 = (q0 - 1 + j) * 128
            pt_, ptk_ = pst.next()
            A("sp", I("dma_start", out=pt_[:], in_=pmine[tok0:tok0 + 128, :]), writes=[ptk_], lane="pstl")
            pb_, pbk_ = pbf.next()
            A("act", I("activation", out=pb_[:], in_=pt_[:], func=AF.Copy), reads=[ptk_], writes=[pbk_])
            ptp, ptpk = pT.next()
            for k in range(2):
                A("pe", I("transpose", ptp[:, k * 128:(k + 1) * 128], pb_[:, k * 128:(k + 1) * 128], idt[:]), reads=[pbk_, "idt"], writes=[ptpk])
            A("act", I("activation", out=pT2[:, :, j * 128:(j + 1) * 128], in_=ptp[:, 0:256].rearrange("p (k c) -> p k c", k=2), func=AF.Copy), reads=[ptpk], writes=["cvT"])
        for hf in range(2):
            wpg, wpgk = wload(w_pg_b[:, :, hf * 512:(hf + 1) * 512], "w_pg")
            wpp, wppk = wload(w_pp_b[:, :, hf * 512:(hf + 1) * 512], "w_pp")
            for j in range(4):
                ap_, k_, _ = hb_tile(sx, j)
                pg_, pgk_ = pP.next()
                for k in range(8):
                    A("pe", I("matmul", pg_[:], lhsT=uT2[:, k, j * 128:(j + 1) * 128], rhs=wpg[:, k, :], start=(k == 0), stop=(k == 7)), reads=[uk2, wpgk], writes=[pgk_])
                pp_, ppk_ = pP.next()
                for k in range(2):
                    A("pe", I("matmul", pp_[:], lhsT=pT2[:, k, j * 128:(j + 1) * 128], rhs=wpp[:, k, :], start=(k == 0), stop=(k == 1)), reads=["cvT", wppk], writes=[ppk_])
                sg, sgk = f5.next()
                A("act", I("activation", out=sg[:], in_=pg_[:], func=AF.Sigmoid), reads=[pgk_], writes=[sgk])
                A("dve", I("tensor_tensor", out=sg[:], in0=pp_[:], in1=sg[:], op=ALU.mult), reads=[ppk_, sgk], writes=[sgk])
                A("dve", I("tensor_tensor", out=ap_[:, hf * 512:(hf + 1) * 512], in0=sg[:], in1=ap_[:, hf * 512:(hf + 1) * 512], op=ALU.add), reads=[sgk, k_], writes=[k_])
        gfnb, gfnk = gain(g_fin)
        for j in range(4):
            tok0 = (q0 - 1 + j) * 128
            ap_, k_, _ = hb_tile(sx, j)
            s1, s1k = st1.next()
            jk_, jkk_ = xn.next()
            A("act", I("activation", out=jk_[:], in_=ap_, func=AF.Square, accum_out=s1[:]), reads=[k_], writes=[jkk_, s1k])
            A("dve", I("tensor_scalar", out=s1[:], in0=s1[:], scalar1=1.0 / 1024, scalar2=EPS, op0=ALU.mult, op1=ALU.add), reads=[s1k], writes=[s1k])
            A("act", I("activation", out=s1[:], in_=s1[:], func=AF.Sqrt), reads=[s1k], writes=[s1k])
            A("dve", I("reciprocal", out=s1[:], in_=s1[:]), reads=[s1k], writes=[s1k])
            o_, ok_ = xs.next()
            A("dve", I("scalar_tensor_tensor", out=o_[:], in0=ap_, scalar=s1[:, 0:1], in1=gfnb[:], op0=ALU.mult, op1=ALU.mult), reads=[k_, s1k, gfnk], writes=[ok_])
            A("sp", I("dma_start", out=out[tok0:tok0 + 128, :], in_=o_[:]), reads=[ok_], lane="ost")

    uTh, ukh = ffn_load_norm(0, 1, 1)
    ffn_up(1, uTh, ukh, True)
    fg = [1 + 4 * i for i in range(4)]
    uT_, uk_ = ffn_load_norm(fg[0], 4, 0)
    ffn_up(4, uT_, uk_, False)
    for gi_, q0 in enumerate(fg):
        sx = gi_ % 2
        ffn_down(sx)
        if gi_ + 1 < len(fg):
            uT_, uk_ = ffn_load_norm(fg[gi_ + 1], 4, 1 - sx)
            ffn_up(4, uT_, uk_, False)
        ple_out(q0, sx)

    return finish_build()


def _host_tables(ty):
    cs = 0 if ty == 1 else 2048
    tabs = np.zeros((NQT, 128, 3, 64), np.float32)
    wmask = np.zeros((NQT, 128, 5, 128), np.float32)
    cmask = np.zeros((NQT, 128, 2, 128), np.float32)
    j = np.arange(64)
    i = np.arange(128)
    for qi in range(NQT):
        T_ = 15 + qi
        cq = T_ * 128 + i
        valid = (64 * j[None, :] <= cq[:, None]) & (64 * j[None, :] >= cs) & (cq[:, None] >= cs)
        cur = cq // 64
        forced = np.full((128, 64), -1000.0, np.float32)
        forced[(j[None, :] == cur[:, None] - 1)] = 1e6
        forced[(j[None, :] == cur[:, None])] = 2e6
        forced[:, cs // 64] = 3e6
        forced[~valid] = -1000.0
        tabs[qi, :, 0, :] = np.where(valid, 0.0, -100.0)
        tabs[qi, :, 1, :] = forced
        tabs[qi, :, 2, :] = valid.astype(np.float32)
        for kt in range(5):
            ck = (T_ - 4 + kt) * 128 + i
            vis = (ck[:, None] >= cs) & (ck[:, None] <= cq[None, :]) & (cq[None, :] - ck[:, None] < 512) & (cq[None, :] >= cs)
            wmask[qi, :, kt, :] = vis
        for nt in range(2):
            nblk = nt * 128 + i
            vis = (nblk[:, None] < 255) & (16 * nblk[:, None] + 31 <= cq[None, :]) & (16 * nblk[:, None] >= cs) & (cq[None, :] >= cs)
            cmask[qi, :, nt, :] = vis
    bf = ml_dtypes.bfloat16
    return tabs, wmask.astype(bf), cmask.astype(bf)


def _host_consts():
    bf = ml_dtypes.bfloat16
    ident = np.eye(128, dtype=np.float32).astype(bf)
    pm = np.zeros((64, 64), np.float32)
    for d2 in range(8):
        pm[d2 + 8, d2] = 1.0
        pm[d2, d2 + 8] = 1.0
    pmbd = np.zeros((128, 128), np.float32)
    pmbd[:64, :64] = pm
    pmbd[64:, 64:] = pm
    ropec = np.zeros((128, 2), np.float32)
    inv = (500000.0 ** (-np.arange(0, 16, 2, dtype=np.float32) / 16)).astype(np.float32)
    for p in range(128):
        d = p % 64
        ropec[p, 0] = inv[d % 8] if d < 16 else 0.0
        ropec[p, 1] = -1.0 if d < 8 else 1.0
    emat = np.zeros((64, 4096), np.float32)
    for jj in range(64):
        emat[jj, jj * 64:(jj + 1) * 64] = 1.0
    nblk = np.arange(256)
    jb = np.arange(64)
    ov = np.clip(np.minimum(nblk[:, None] * 16 + 32, jb[None, :] * 64 + 64) - np.maximum(nblk[:, None] * 16, jb[None, :] * 64), 0, None).astype(np.float32) / 16
    ov[255] = 0
    ovm = np.zeros((128, 2, 65), np.float32)
    ovm[:, 0, :64] = ov[:128]
    ovm[:, 1, :64] = ov[128:]
    ovm[:, :, 64] = 1.0
    ovm[127, 1, 64] = 0.0
    return dict(ident=ident, pmbd=pmbd.astype(bf), ropec=ropec, emat=emat.astype(bf), ovm=ovm.astype(bf))


def _kc(w, kc):
    return np.ascontiguousarray(w.reshape(kc, 128, w.shape[1]).transpose(1, 0, 2))


_DBG = ()
_STOP = 99


def kernel(**inp):
    f = lambda a: np.ascontiguousarray(np.asarray(a))
    x = f(inp["x"]); p = f(inp["p"]); positions = f(inp["positions"]).astype(np.int32)
    shared = dict(
        w_in=_kc(f(inp["w_in"])[0], 8),
        g_mix=f(inp["norm_mix_g"]).reshape(1, 1024), g_ffn=f(inp["norm_ffn_g"]).reshape(1, 1024),
        g_ple=f(inp["norm_ple_g"]).reshape(1, 1024), g_fin=f(inp["norm_final_g"]).reshape(1, 1024),
        w1k=np.ascontiguousarray(f(inp["cmp_k_w1"])[0].reshape(32, 64, 128).transpose(1, 0, 2)),
        w1v=np.ascontiguousarray(f(inp["cmp_v_w1"])[0].reshape(32, 64, 128).transpose(1, 0, 2)),
        w2k=f(inp["cmp_k_w2"])[0], w2v=f(inp["cmp_v_w2"])[0],
        pek=np.ascontiguousarray(f(inp["pe_k"])[0].T), pev=np.ascontiguousarray(f(inp["pe_v"])[0].T),
        w_o=_kc(f(inp["w_o"])[0], 8),
        w_pg=_kc(f(inp["w_ple_gate"])[0], 8), w_pp=_kc(f(inp["w_ple_proj"])[0], 2),
    )
    cd = np.concatenate([f(inp["conv_dw_w"])[0, :, 0, :], f(inp["conv_dw_b"]), f(inp["conv_ln_g"]), f(inp["conv_ln_b"])], 0)
    shared["cdw"] = np.ascontiguousarray(cd.reshape(34, 4, 128).transpose(2, 1, 0))
    fd = np.concatenate([f(inp["ffn_dw_w"])[0, :, 0, :], f(inp["ffn_dw_b"])], 0)
    shared["fdw"] = np.ascontiguousarray(fd.reshape(4, 44, 128).transpose(2, 1, 0))
    war, wbr, wir = _kc(f(inp["w_a"])[0], 4), _kc(f(inp["w_b"])[0], 4), shared["w_in"]
    wq_ = wir[:, :, 0:512].reshape(128, 8, 2, 4, 64).transpose(0, 1, 3, 2, 4).reshape(128, 8, 512)
    wir = wir.copy()
    wir[:, :, 0:512] = wq_
    shared["w_in"] = wir
    wm = np.empty((128, 8, 3072), np.float32)
    for dc in range(8):
        wm[:, dc, 0:512] = war[:, :, dc * 128:(dc + 1) * 128].reshape(128, 512)
        wm[:, dc, 512:1024] = wbr[:, :, dc * 128:(dc + 1) * 128].reshape(128, 512)
        wm[:, dc, 1024:2048] = wir[:, :, 2328 + dc * 128:2328 + (dc + 1) * 128].reshape(128, 1024)
        wm[:, dc, 2048:3072] = wir[:, :, 3352 + dc * 128:3352 + (dc + 1) * 128].reshape(128, 1024)
    shared["wmrg"] = wm
    wur = _kc(f(inp["w_up"])[0], 8)
    wu = np.empty((128, 22, 8, 256), np.float32)
    for jc in range(22):
        wu[:, jc, :, 0:128] = wur[:, :, jc * 128:(jc + 1) * 128]
        wu[:, jc, :, 128:256] = wur[:, :, 2816 + jc * 128:2816 + (jc + 1) * 128]
    shared["w_up"] = wu.reshape(128, 22, 2048)
    wdr = _kc(f(inp["w_down"])[0], 22)
    shared["w_dn"] = np.ascontiguousarray(wdr.reshape(128, 22, 2, 512).transpose(0, 2, 1, 3))
    shared.update(_host_consts())
    tables = {ty: _host_tables(ty) for ty in (0, 1)}
    in_maps = []
    for c in range(8):
        b, ty = c // 2, c % 2
        m = dict(shared)
        if ty == 1:
            m["xc"] = x[b]
            m["posr"] = positions[b].reshape(1, 4096)
        else:
            m["xc"] = np.concatenate([np.zeros((2048, 1024), np.float32), x[b, :2048]], 0)
            m["posr"] = np.concatenate([np.zeros(2048, np.int32), positions[b, :2048]]).reshape(1, 4096)
        m["posc"] = np.ascontiguousarray(m["posr"][:, 31:4096:16])
        m["pmine"] = np.ascontiguousarray(p[0, b, ty * 2048:(ty + 1) * 2048])
        m["tabs"], m["wmask"], m["cmaskd"] = tables[ty]
        m["flagd"] = np.full((128, 1), float(ty), np.float32)
        in_maps.append(m)
    nc = bass.Bass("TRN2", target_bir_lowering=False)
    build(nc, _DBG, _STOP)
    res = run_bass_kernel_spmd(nc, in_maps, core_ids=list(range(8)))
    outp = np.zeros((4, 4096, 1024), np.float32)
    for c in range(8):
        b, ty = c // 2, c % 2
        outp[b, ty * 2048:(ty + 1) * 2048] = res.results[c]["out"]
    kernel.last = res
    if _DBG:
        kernel.dbg = [{n_: res.results[c]["dbg_" + n_] for n_, _ in _DBG} for c in range(8)]
    return outp
```
